# Optimizing a Trainium2 kernel written in Bass

```python
import math
import jax, jax.numpy as jnp
from jax import lax
import numpy as np

D_MODEL = 4096
BATCH = 2
SEQ = 4096
DEPTH = 2

HEAD_DIM = 128
A_HEADS = 8
A_Q_LORA = 768
A_KV_LORA = 512
A_NOPE = 128
A_ROPE = 64
A_V = 128
B_PAIRS = ((128, 1), (512, 4), (2048, 16))
B_HEADS_PER_PAIR = 4
B_HEADS = B_HEADS_PER_PAIR * len(B_PAIRS)
C_HEADS = 8
C_KV_HEADS = 2
D_HEADS = 8
D_KV_HEADS = 2
D_WINDOW = 128

GRID_W = 64
ROPE_THETA = 10000.0
N_BUCKETS = 32
REL_MAX_DIST = 1024
BLOCK = 128
EPS = 1e-6
NEG = -1e30

W_A = A_HEADS * A_V
W_B = B_HEADS * HEAD_DIM
W_C = C_HEADS * HEAD_DIM
W_D = D_HEADS * HEAD_DIM
MIX_WIDTH = W_A + W_B + W_C + W_D
IN_SPLITS = (A_Q_LORA, A_KV_LORA, A_ROPE,
             W_B, W_B, W_B,
             W_C, C_KV_HEADS * HEAD_DIM, C_KV_HEADS * HEAD_DIM,
             W_D, D_KV_HEADS * HEAD_DIM, D_KV_HEADS * HEAD_DIM,
             MIX_WIDTH)
IN_WIDTH = sum(IN_SPLITS)

kernel_name = "hymba_style_mla_dilated_axial_sink_encoder"


def rmsnorm(x, g):
    xf = x.astype(jnp.float32)
    y = xf * lax.rsqrt(jnp.mean(xf * xf, axis=-1, keepdims=True) + EPS)
    return (y * g.astype(jnp.float32)).astype(x.dtype)


def rope_angles(pos, dim):
    inv = ROPE_THETA ** (-jnp.arange(0, dim, 2, dtype=jnp.float32) / dim)
    return pos.astype(jnp.float32)[:, None] * inv[None, :]


def apply_rope(x, ang):
    xf = x.astype(jnp.float32)
    half = x.shape[-1] // 2
    x1, x2 = xf[..., :half], xf[..., half:]
    cos = jnp.cos(ang)[:, None, :]
    sin = jnp.sin(ang)[:, None, :]
    return jnp.concatenate([x1 * cos - x2 * sin, x2 * cos + x1 * sin], axis=-1).astype(x.dtype)


def rel_bucket(rel):
    nb = N_BUCKETS // 2
    max_exact = nb // 2
    ret = jnp.where(rel > 0, nb, 0)
    n = jnp.abs(rel)
    nf = jnp.maximum(n, 1).astype(jnp.float32)
    large = max_exact + (jnp.log(nf / max_exact) / math.log(REL_MAX_DIST / max_exact)
                         * (nb - max_exact)).astype(jnp.int32)
    large = jnp.minimum(large, nb - 1)
    return ret + jnp.where(n < max_exact, n, large)


def dense_attention(q, k, v, scale):
    b, s = q.shape[:2]
    nblk = s // BLOCK
    qb = jnp.moveaxis(q.reshape(b, nblk, BLOCK, *q.shape[2:]), 1, 0)

    def one(qblk):
        logits = jnp.einsum('bqhgd,bkhd->bhgqk', qblk, k).astype(jnp.float32) * scale
        p = jax.nn.softmax(logits, axis=-1).astype(v.dtype)
        return jnp.einsum('bhgqk,bkhd->bqhgd', p, v)

    out = lax.map(one, qb)
    return jnp.moveaxis(out, 0, 1).reshape(b, s, *out.shape[3:])


def dilated_group(q, k, v, dil, side, bias_tab):
    b, s, h, d = q.shape
    offs = dil * jnp.arange(-side, side + 1)
    bias = bias_tab[rel_bucket(offs)].T.astype(jnp.float32)
    nblk = s // BLOCK
    qb = jnp.moveaxis(q.reshape(b, nblk, BLOCK, h, d), 1, 0)
    starts = jnp.arange(nblk) * BLOCK
    scale = d ** -0.5

    def one(args):
        qblk, st = args
        idx = (st + jnp.arange(BLOCK))[:, None] + offs[None, :]
        valid = (idx >= 0) & (idx < s)
        idxc = jnp.clip(idx, 0, s - 1)
        kg = k[:, idxc]
        vg = v[:, idxc]
        logits = jnp.einsum('bqhd,bqkhd->bhqk', qblk, kg).astype(jnp.float32) * scale + bias[None, :, None, :]
        logits = jnp.where(valid[None, None], logits, NEG)
        lse = jax.nn.logsumexp(logits, axis=-1)
        p = jnp.exp(logits - lse[..., None]).astype(v.dtype)
        return jnp.einsum('bhqk,bqkhd->bqhd', p, vg), lse

    o, lse = lax.map(one, (qb, starts))
    o = jnp.moveaxis(o, 0, 1).reshape(b, s, h, d)
    lse = jnp.transpose(lse, (1, 0, 3, 2)).reshape(b, s, h)
    return o, lse


def window_attention(q, k, v, sinks, bias_tab):
    b, s, hk, g, d = q.shape
    nblk = s // BLOCK
    pad = ((0, 0), (BLOCK, BLOCK), (0, 0), (0, 0))
    kp = jnp.pad(k, pad).reshape(b, nblk + 2, BLOCK, hk, d)
    vp = jnp.pad(v, pad).reshape(b, nblk + 2, BLOCK, hk, d)
    kband = jnp.concatenate([kp[:, :-2], kp[:, 1:-1], kp[:, 2:]], axis=2)
    vband = jnp.concatenate([vp[:, :-2], vp[:, 1:-1], vp[:, 2:]], axis=2)
    qb = q.reshape(b, nblk, BLOCK, hk, g, d)
    kj = jnp.arange(3 * BLOCK)[None, :] - BLOCK
    rel = kj - jnp.arange(BLOCK)[:, None]
    kpos = jnp.arange(nblk)[:, None] * BLOCK + kj
    valid = (jnp.abs(rel) <= D_WINDOW)[None] & ((kpos >= 0) & (kpos < s))[:, None, :]
    bias = jnp.moveaxis(bias_tab[rel_bucket(rel)], -1, 0).reshape(hk, g, BLOCK, 3 * BLOCK).astype(jnp.float32)
    logits = jnp.einsum('bnqhgd,bnkhd->bnhgqk', qb, kband).astype(jnp.float32) * (d ** -0.5) + bias
    logits = jnp.where(valid[None, :, None, None], logits, NEG)
    sink = jnp.broadcast_to(sinks.astype(jnp.float32).reshape(hk, g)[None, None, :, :, None, None],
                            logits.shape[:-1] + (1,))
    p = jax.nn.softmax(jnp.concatenate([logits, sink], axis=-1), axis=-1)[..., :-1].astype(v.dtype)
    o = jnp.einsum('bnhgqk,bnkhd->bnqhgd', p, vband)
    return o.reshape(b, s, hk, g, d)


def hybrid_layer(x, g_attn, w_in, g_qa, g_kva, w_uq, w_ukv, g_qn, g_kn, sinks, w_out,
                 rel_bias, ang_t, ang_row, ang_col):
    b, s, _ = x.shape
    h = rmsnorm(x, g_attn)
    proj = h @ w_in
    pts = [int(p) for p in np.cumsum(IN_SPLITS)[:-1]]
    (a_cq, a_ckv, a_kpe, b_q, b_k, b_v, c_q, c_k, c_v, d_q, d_k, d_v, gate) = jnp.split(proj, pts, axis=-1)

    qa = (rmsnorm(a_cq, g_qa) @ w_uq).reshape(b, s, A_HEADS, A_NOPE + A_ROPE)
    q_pe = apply_rope(qa[..., A_NOPE:], ang_t)
    kva = (rmsnorm(a_ckv, g_kva) @ w_ukv).reshape(b, s, A_HEADS, A_NOPE + A_V)
    k_pe = apply_rope(a_kpe[:, :, None, :], ang_t)
    q_a = jnp.concatenate([qa[..., :A_NOPE], q_pe], axis=-1)[:, :, :, None, :]
    k_a = jnp.concatenate([kva[..., :A_NOPE], jnp.broadcast_to(k_pe, (b, s, A_HEADS, A_ROPE))], axis=-1)
    o_a = dense_attention(q_a, k_a, kva[..., A_NOPE:], (A_NOPE + A_ROPE) ** -0.5).reshape(b, s, W_A)

    npair = len(B_PAIRS)
    b_q = b_q.reshape(b, s, npair, B_HEADS_PER_PAIR, HEAD_DIM)
    b_k = b_k.reshape(b, s, npair, B_HEADS_PER_PAIR, HEAD_DIM)
    b_v = b_v.reshape(b, s, npair, B_HEADS_PER_PAIR, HEAD_DIM)
    outs, lses = [], []
    for j, (win, dil) in enumerate(B_PAIRS):
        o, lse = dilated_group(b_q[:, :, j], b_k[:, :, j], b_v[:, :, j], dil, win // (2 * dil),
                               rel_bias[:, j * B_HEADS_PER_PAIR:(j + 1) * B_HEADS_PER_PAIR])
        outs.append(o)
        lses.append(lse)
    alpha = jax.nn.softmax(jnp.stack(lses, axis=2), axis=2)
    o_b = (jnp.stack(outs, axis=2) * alpha[..., None].astype(x.dtype)).reshape(b, s, W_B)

    half = HEAD_DIM // 2
    qc = rmsnorm(c_q.reshape(b, s, C_HEADS, HEAD_DIM), g_qn)
    kc = rmsnorm(c_k.reshape(b, s, C_KV_HEADS, HEAD_DIM), g_kn)
    qc = jnp.concatenate([apply_rope(qc[..., :half], ang_row), apply_rope(qc[..., half:], ang_col)], axis=-1)
    kc = jnp.concatenate([apply_rope(kc[..., :half], ang_row), apply_rope(kc[..., half:], ang_col)], axis=-1)
    o_c = dense_attention(qc.reshape(b, s, C_KV_HEADS, C_HEADS // C_KV_HEADS, HEAD_DIM), kc,
                          c_v.reshape(b, s, C_KV_HEADS, HEAD_DIM), HEAD_DIM ** -0.5).reshape(b, s, W_C)

    o_d = window_attention(d_q.reshape(b, s, D_KV_HEADS, D_HEADS // D_KV_HEADS, HEAD_DIM),
                           d_k.reshape(b, s, D_KV_HEADS, HEAD_DIM),
                           d_v.reshape(b, s, D_KV_HEADS, HEAD_DIM),
                           sinks, rel_bias[:, B_HEADS:]).reshape(b, s, W_D)

    mixed = jnp.concatenate([o_a, o_b, o_c, o_d], axis=-1) * jax.nn.silu(gate)
    return x + mixed @ w_out


def setup_inputs(seed: int = 0) -> dict:
    key = jax.random.key(seed)
    ks = jax.random.split(key, 16)
    f32 = jnp.float32
    nrm = lambda k, shape, scale: jax.random.normal(k, shape, f32) * scale
    return {
        "x": nrm(ks[0], (BATCH, SEQ, D_MODEL), 1.0),
        "g_attn": 1.0 + nrm(ks[1], (DEPTH, D_MODEL), 0.02),
        "w_in": nrm(ks[2], (DEPTH, D_MODEL, IN_WIDTH), D_MODEL ** -0.5),
        "g_qa": 1.0 + nrm(ks[3], (DEPTH, A_Q_LORA), 0.02),
        "g_kva": 1.0 + nrm(ks[4], (DEPTH, A_KV_LORA), 0.02),
        "w_uq": nrm(ks[5], (DEPTH, A_Q_LORA, A_HEADS * (A_NOPE + A_ROPE)), A_Q_LORA ** -0.5),
        "w_ukv": nrm(ks[6], (DEPTH, A_KV_LORA, A_HEADS * (A_NOPE + A_V)), A_KV_LORA ** -0.5),
        "g_qn": 1.0 + nrm(ks[7], (DEPTH, HEAD_DIM), 0.02),
        "g_kn": 1.0 + nrm(ks[8], (DEPTH, HEAD_DIM), 0.02),
        "sinks": nrm(ks[9], (DEPTH, D_HEADS), 0.5),
        "w_out": nrm(ks[10], (DEPTH, MIX_WIDTH, D_MODEL), MIX_WIDTH ** -0.5),
        "rel_bias": nrm(ks[11], (N_BUCKETS, B_HEADS + D_HEADS), 0.1),
        "g_final": 1.0 + nrm(ks[12], (D_MODEL,), 0.02),
    }


def reference(x, g_attn, w_in, g_qa, g_kva, w_uq, w_ukv, g_qn, g_kn, sinks, w_out, rel_bias, g_final):
    s = x.shape[1]
    rows = s // GRID_W
    ang_t = rope_angles(jnp.arange(s), A_ROPE)
    ang_row = rope_angles(jnp.repeat(jnp.arange(rows), GRID_W), HEAD_DIM // 2)
    ang_col = rope_angles(jnp.tile(jnp.arange(GRID_W), rows), HEAD_DIM // 2)
    for l in range(DEPTH):
        x = hybrid_layer(x, g_attn[l], w_in[l], g_qa[l], g_kva[l], w_uq[l], w_ukv[l], g_qn[l], g_kn[l],
                         sinks[l], w_out[l], rel_bias, ang_t, ang_row, ang_col)
    return rmsnorm(x, g_final)
```

```python
import contextlib
import math
import numpy as np
import ml_dtypes
from concourse.bass_utils import run_bass_kernel_spmd
import concourse.bass as bass
import concourse.mybir as mybir

F32 = mybir.dt.float32
BF16 = mybir.dt.bfloat16
ALU = mybir.AluOpType
AF = mybir.ActivationFunctionType
AX = mybir.AxisListType

ENGS = ["pe", "act", "dve", "pool", "sp"]


class Tile:
    def __init__(self, name, t):
        self.name = name
        self.t = t
        self.writers = []
        self.readers = []
        self.dsem = None

    def __getitem__(self, idx):
        return self.t[idx]


class Op:
    __slots__ = ("eng", "fn", "waits", "signals", "sigval", "dma", "idx")


class Prog:
    def __init__(self, nc, same_engine_sync=True):
        self.nc = nc
        self.stack = contextlib.ExitStack()
        self.ops = {e: [] for e in ENGS}
        self.same_engine_sync = same_engine_sync
        self.dsems = []
        self.esem = {}
        self.ntile = 0

    def sem(self, name):
        return self.stack.enter_context(self.nc.semaphore(name))

    def sbuf(self, name, shape, dt):
        t = self.stack.enter_context(self.nc.sbuf_tensor(name, list(shape), dt))
        return Tile(name, t)

    def psum(self, name, shape, dt):
        t = self.stack.enter_context(self.nc.psum_tensor(name, list(shape), dt))
        return Tile(name, t)

    def dram(self, name, shape, dt, kind=None):
        if kind is None:
            t = self.nc.dram_tensor(name, list(shape), dt)
        else:
            t = self.nc.dram_tensor(name, list(shape), dt, kind=kind)
        return Tile(name, t)

    def sub(self, tile, name=None):
        return Tile(name or tile.name, tile.t)

    def _deps(self, eng, reads, writes):
        toks = []
        for t in reads:
            toks += t.writers
        for t in writes:
            toks += t.writers
            toks += t.readers
        waits = []
        for tok in toks:
            if tok[0] == "E":
                if tok[1] == eng and (eng == "pe" or not self.same_engine_sync):
                    continue
                waits.append(tok)
            else:
                waits.append(("D", tok[1], self.dsems[tok[1]][1]))
        return waits

    def _commit(self, tok, reads, writes):
        for t in reads:
            t.readers.append(tok)
        for t in writes:
            t.writers = [tok]
            t.readers = []

    def op(self, eng, fn, reads=(), writes=()):
        o = Op()
        o.eng = eng
        o.fn = fn
        o.dma = False
        o.signals = False
        o.sigval = None
        o.waits = self._deps(eng, reads, writes)
        o.idx = len(self.ops[eng])
        self.ops[eng].append(o)
        self._commit(("E", eng, o.idx), reads, writes)
        return o

    def dma(self, eng, out_ap, in_ap, reads=(), writes=(), semtile=None, **kw):
        if semtile is None:
            semtile = (list(writes) + list(reads))[0]
        if semtile.dsem is None:
            semtile.dsem = len(self.dsems)
            self.dsems.append([self.sem("d_" + semtile.name + str(len(self.dsems))), 0])
        sid = semtile.dsem
        o = Op()
        o.eng = eng
        o.dma = True
        o.signals = False
        o.waits = self._deps(eng, reads, writes)
        self.dsems[sid][1] += 16
        val = self.dsems[sid][1]
        semh = self.dsems[sid][0]

        def fn(e, out_ap=out_ap, in_ap=in_ap, semh=semh, kw=kw):
            e.dma_start(out=out_ap, in_=in_ap, **kw).then_inc(semh, 16)

        o.fn = fn
        o.sigval = None
        o.idx = len(self.ops[eng])
        self.ops[eng].append(o)
        self._commit(("D", sid, val), reads, writes)
        return o

    def emit(self, final_waits_engine="sp"):
        nc = self.nc
        for e in ENGS:
            for o in self.ops[e]:
                for w in o.waits:
                    if w[0] == "E":
                        self.ops[w[1]][w[2]].signals = True
        for e in ENGS:
            c = 0
            for o in self.ops[e]:
                if o.signals:
                    c += 1
                    o.sigval = c
            self.esem[e] = self.sem("e_" + e)
        final = [(s[0], s[1]) for s in self.dsems if s[1] > 0]

        def replay(ename, eng):
            seen = {}
            for o in self.ops[ename]:
                for w in o.waits:
                    if w[0] == "E":
                        semh = self.esem[w[1]]
                        key = ("E", w[1])
                        val = self.ops[w[1]][w[2]].sigval
                    else:
                        semh = self.dsems[w[1]][0]
                        key = ("D", w[1])
                        val = w[2]
                    if seen.get(key, 0) >= val:
                        continue
                    seen[key] = val
                    eng.wait_ge(semh, val)
                if o.dma:
                    o.fn(eng)
                else:
                    ins = o.fn(eng)
                    if o.signals:
                        ins.then_inc(self.esem[ename], 1)
            if ename == final_waits_engine:
                for semh, val in final:
                    eng.wait_ge(semh, val)

        with nc.Block() as block:
            @block.tensor
            def _(e):
                replay("pe", e)

            @block.scalar
            def _(e):
                replay("act", e)

            @block.vector
            def _(e):
                replay("dve", e)

            @block.gpsimd
            def _(e):
                replay("pool", e)

            @block.sync
            def _(e):
                replay("sp", e)

    def close(self):
        self.stack.close()


    def g(self, eng, method, reads, writes, *args, **kw):
        return self.op(eng, lambda e: getattr(e, method)(*args, **kw), reads, writes)

    def mm(self, out, lhsT, rhs, start, stop, reads, writes):
        return self.op("pe", lambda e: e.matmul(out, lhsT=lhsT, rhs=rhs, start=start, stop=stop), reads, writes)

    def tr(self, out, in_, ident, reads, writes):
        return self.op("pe", lambda e: e.transpose(out, in_, ident), reads, writes)


S_LOC = 1024
NT = S_LOC // 128
NW = 107
EPS = 1e-6
T_CQ = (0, 6)
T_CKV = (6, 10)
T_KPE = 10
T_BQ = (11, 23)
T_BK = (23, 35)
T_BV = (35, 47)
T_CQh = (47, 55)
T_CK = (55, 57)
T_CV = (57, 59)
T_DQ = (59, 67)
T_DK = (67, 69)
T_DV = (69, 71)
T_G = (71, 107)


def build_P(stage=99):
    nc = bass.Bass("TRN2", target_bir_lowering=False)
    P = Prog(nc)
    IN = "ExternalInput"
    OUT = "ExternalOutput"
    x = P.dram("x", [S_LOC, 4096], F32, IN)
    wt = P.dram("wt", [NW if stage >= 0 else 1, 128, 4096], F32, IN)
    gattn = P.dram("gattn", [128, 32], F32, IN)
    gqa = P.dram("gqa", [128, 6], F32, IN)
    gkva = P.dram("gkva", [128, 4], F32, IN)
    gqn = P.dram("gqn", [128, 1], F32, IN)
    gkn = P.dram("gkn", [128, 1], F32, IN)
    wuq = P.dram("wuq", [128, 6 * 1536], F32, IN)
    wukv = P.dram("wukv", [128, 4 * 2048], F32, IN)
    cosA = P.dram("cosA", [64, S_LOC], F32, IN)
    sinA = P.dram("sinA", [64, S_LOC], F32, IN)
    cosC = P.dram("cosC", [128, S_LOC], F32, IN)
    sinC = P.dram("sinC", [128, S_LOC], F32, IN)

    QA = P.dram("QA", [8, 192, S_LOC], BF16, OUT)
    KA = P.dram("KA", [8, 128, S_LOC], BF16, OUT)
    KPE = P.dram("KPE", [64, S_LOC], BF16, OUT)
    VA = P.dram("VA", [S_LOC, 1024], BF16, OUT)
    QB = P.dram("QB", [12, 128, S_LOC], BF16, OUT)
    KB = P.dram("KB", [12, 128, S_LOC], BF16, OUT)
    VB = P.dram("VB", [S_LOC, 1536], BF16, OUT)
    QC = P.dram("QC", [8, 128, S_LOC], BF16, OUT)
    KC = P.dram("KC", [2, 128, S_LOC], BF16, OUT)
    VC = P.dram("VC", [S_LOC, 256], BF16, OUT)
    QD = P.dram("QD", [8, 128, S_LOC], BF16, OUT)
    KD = P.dram("KD", [2, 128, S_LOC], BF16, OUT)
    VD = P.dram("VD", [S_LOC, 256], BF16, OUT)
    G = P.dram("G", [36, 128, S_LOC], BF16, OUT)

    hT = P.sbuf("hT", [128, 32, S_LOC], BF16)
    NB = 3
    ring = [P.sbuf(f"wr{i}", [128, 32, 128], BF16) for i in range(NB)]
    wup = P.sbuf("wup", [128, 6 * 1536], BF16)
    lat = P.sbuf("lat", [128, 6, S_LOC], F32)
    latn = P.sbuf("latn", [128, 6, S_LOC], BF16)
    tcos = P.sbuf("tcos", [128, S_LOC], F32)
    tsin = P.sbuf("tsin", [128, S_LOC], F32)
    sq = [P.sbuf(f"sq{i}", [128, 512], BF16) for i in range(2)]
    rs = P.sbuf("rs", [128, S_LOC], F32)
    xf = [P.sbuf(f"xf{i}", [128, 512], F32) for i in range(2)]
    xs = [P.sbuf(f"xs{i}", [128, 512], F32) for i in range(2)]
    t1 = [P.sbuf(f"t1{i}", [128, 512], F32) for i in range(2)]
    t2 = [P.sbuf(f"t2{i}", [128, 512], F32) for i in range(2)]
    NSTG = 3
    stg = [P.sbuf(f"stg{i}", [128, S_LOC], BF16) for i in range(NSTG)]
    vst = [P.sbuf(f"vst{i}", [128, 8, 128], BF16) for i in range(2)]
    ident = P.sbuf("ident", [128, 128], BF16)
    identf = P.sbuf("identf", [128, 128], F32)
    ones = P.sbuf("ones", [128, 128], BF16)
    epst = P.sbuf("epst", [128, 1], F32)
    ssq = P.sbuf("ssq", [128, 1], F32)
    ssq8 = P.sbuf("ssq8", [128, 8], F32)
    junk = P.sbuf("junk", [128, 512], F32)
    rstd = P.sbuf("rstd", [128, 1], F32)
    gat = P.sbuf("gat", [128, 32], F32)
    gqa_t = P.sbuf("gqa_t", [128, 6], F32)
    gkva_t = P.sbuf("gkva_t", [128, 4], F32)
    gqn_t = P.sbuf("gqn_t", [128, 1], F32)
    gkn_t = P.sbuf("gkn_t", [128, 1], F32)
    pj = [P.psum(f"pj{i}", [128, 512], F32) for i in range(4)]
    tp = [P.psum(f"tp{i}", [128, 8, 128], BF16) for i in range(2)]
    sp_ = [P.psum(f"ssqp{i}", [128, 512], F32) for i in range(2)]

    P.g("pool", "memset", [], [identf], identf[:], 1.0)
    P.g("pool", "affine_select", [identf], [identf], out=identf[:], in_=identf[:], pattern=[[-1, 128]],
        compare_op=ALU.is_equal, fill=0.0, base=0, channel_multiplier=1)
    P.g("dve", "tensor_copy", [identf], [ident], out=ident[:], in_=identf[:])
    P.g("pool", "memset", [], [ones], ones[:], 1.0)
    P.g("pool", "memset", [], [epst], epst[:], EPS)
    for dst, src in ((gat, gattn), (gqa_t, gqa), (gkva_t, gkva), (gqn_t, gqn), (gkn_t, gkn)):
        P.dma("sp", dst[:], src[:], [], [dst])

    hsub = [[P.sub(hT, f"hT{tt}a"), P.sub(hT, f"hT{tt}b")] for tt in range(NT)]
    xt = lat
    xt_ap = lat[:, 0:4, :].rearrange("p a b -> p (a b)")
    xn_ap = latn[:, 0:4, :].rearrange("p a b -> p (a b)")
    junk_ap = lat[:, 4:6, :].rearrange("p a b -> p (a b)")
    ntp = 0
    if stage == -2:
        for tt in range(NT):
            P.g('pool', 'memset', [], hsub[tt], hT[:, :, tt * 128:(tt + 1) * 128], 0.5)
    for tt in range(NT if stage != -2 else 0):
        P.dma("sp", xt_ap, x[tt * 128:(tt + 1) * 128, :], [], [lat])
        P.g("dve", "memset", [], [ssq8], ssq8[:], 0.0)
        for k in range(8):
            P.g("act", "activation", [lat, ssq8], [junk, ssq8], out=junk[:, :], in_=xt_ap[:, k * 512:(k + 1) * 512], func=AF.Square,
                accum_out=ssq8[:, k:k + 1])
        P.g("dve", "reduce_sum", [ssq8], [ssq], out=ssq[:, 0:1], in_=ssq8[:, 0:8], axis=AX.X)
        P.g("act", "activation", [ssq, epst], [rstd], out=rstd[:], in_=ssq[:], func=AF.Sqrt, bias=epst[:], scale=1.0 / 4096)
        P.g("dve", "reciprocal", [rstd], [rstd], out=rstd[:], in_=rstd[:])
        P.g("dve", "tensor_scalar", [lat, rstd], [latn], out=xn_ap, in0=xt_ap, scalar1=rstd[:, 0:1], scalar2=None, op0=ALU.mult)
        for grp in range(4):
            tpt = tp[ntp % 2]
            ntp += 1
            for j in range(8):
                c = grp * 8 + j
                P.tr(tpt[:, j, :], xn_ap[:, c * 128:(c + 1) * 128], ident[:], [latn, ident], [tpt])
            for j in range(8):
                c = grp * 8 + j
                dst = hT[:, c, tt * 128:(tt + 1) * 128]
                if j % 2 == 0:
                    P.g("dve", "tensor_scalar", [tpt, gat], [hsub[tt][0]], out=dst, in0=tpt[:, j, :],
                        scalar1=gat[:, c:c + 1], scalar2=None, op0=ALU.mult)
                else:
                    P.g("pool" if False else "dve", "tensor_scalar", [tpt, gat], [hsub[tt][1]], out=dst, in0=tpt[:, j, :],
                        scalar1=gat[:, c:c + 1], scalar2=None, op0=ALU.mult)

    if stage == -1:
        P.dma('sp', G[0, :, :], hT[:, 0, :], [t for tt in range(NT) for t in hsub[tt]], [])
        P.emit(); P.close(); return nc
    def hreads(tc):
        r = []
        for tt in range(tc * 4, tc * 4 + 4):
            r += hsub[tt]
        return r

    state = {"k": 0, "stg": 0, "pj": 0, "vst": 0, "tp": ntp, "alt": 0}

    def proj(widx, M, epi):
        slot = ring[state["k"] % NB]
        state["k"] += 1
        P.dma("pool", slot[:, :, :], wt[widx].rearrange("p (c n) -> p c n", c=32), [], [slot])
        for tc in range(2):
            ps = pj[state["pj"] % 4]
            state["pj"] += 1
            hr = hreads(tc)
            for c in range(32):
                P.mm(ps[0:M, :], slot[:, c, 0:M], hT[:, c, tc * 512:(tc + 1) * 512], c == 0, c == 31, [slot] + hr, [ps])
            epi(tc, ps)

    def upproj(src, nchunk, wcols, col0, M, epi):
        for tc in range(2):
            ps = pj[state["pj"] % 4]
            state["pj"] += 1
            for c in range(nchunk):
                P.mm(ps[0:M, :], wup[:, c * wcols + col0: c * wcols + col0 + M], src[:, c, tc * 512:(tc + 1) * 512],
                     c == 0, c == nchunk - 1, [wup, src], [ps])
            epi(tc, ps)

    def next_stg():
        s = stg[state["stg"] % NSTG]
        state["stg"] += 1
        return s

    def alt_copy(out, in_, scale, reads, writes):
        state["alt"] += 1
        if state["alt"] % 2 == 0:
            P.g("dve", "tensor_scalar", reads, writes, out=out, in0=in_, scalar1=float(scale), scalar2=None, op0=ALU.mult)
        else:
            P.g("dve", "tensor_scalar", reads, writes, out=out, in0=in_, scalar1=float(scale), scalar2=None, op0=ALU.mult)

    def epi_plain(dst_ap, scale, M=128):
        s = next_stg()

        def epi(tc, ps):
            alt_copy(s[0:M, tc * 512:(tc + 1) * 512], ps[0:M, :], scale, [ps], [s])
            if tc == 1:
                P.dma("sp", dst_ap, s[0:M, :], [s], [])
        return epi

    def epi_silu(dst_ap):
        s = next_stg()

        def epi(tc, ps):
            P.g("act", "activation", [ps], [s], out=s[:, tc * 512:(tc + 1) * 512], in_=ps[:, :], func=AF.Silu)
            if tc == 1:
                P.dma("sp", dst_ap, s[:, :], [s], [])
        return epi

    def epi_v(dst_dram, col0):
        s = next_stg()

        def epi(tc, ps):
            alt_copy(s[:, tc * 512:(tc + 1) * 512], ps[:, :], 1.0, [ps], [s])
            if tc == 1:
                tpt = tp[state["tp"] % 2]
                state["tp"] += 1
                for b in range(8):
                    P.tr(tpt[:, b, :], s[:, b * 128:(b + 1) * 128], ident[:], [s, ident], [tpt])
                vs = vst[state["vst"] % 2]
                state["vst"] += 1
                P.g("dve", "tensor_copy", [tpt], [vs], out=vs[:, :, :], in_=tpt[:, :, :])
                P.dma("sp", dst_dram[:, col0:col0 + 128].rearrange("(b p) d -> p b d", p=128), vs[:, :, :], [vs], [])
        return epi

    def epi_latent(c, n, ndim):
        def epi(tc, ps):
            sl = slice(tc * 512, (tc + 1) * 512)
            P.g("dve", "tensor_copy", [ps], [lat], out=lat[:, c, sl], in_=ps[:, :])
            sqt = sq[tc]
            P.g("act", "activation", [lat], [sqt], out=sqt[:, :], in_=lat[:, c, sl], func=AF.Square)
            P.mm(sp_[tc][:, :], ones[:, :], sqt[:, :], c == 0, c == n - 1, [ones, sqt], [sp_[tc]])
        return epi

    def latent_norm(n, ndim, gt):
        for tc in range(2):
            sl = slice(tc * 512, (tc + 1) * 512)
            P.g("act", "activation", [sp_[tc], epst], [rs], out=rs[:, sl], in_=sp_[tc][:, :], func=AF.Sqrt, bias=epst[:],
                scale=1.0 / ndim)
            P.g("dve", "reciprocal", [rs], [rs], out=rs[:, sl], in_=rs[:, sl])
            for c in range(n):
                P.g("dve", "tensor_scalar", [lat, gt], [lat], out=lat[:, c, sl], in0=lat[:, c, sl], scalar1=gt[:, c:c + 1], scalar2=None, op0=ALU.mult)
                P.g("dve", "tensor_tensor", [lat, rs], [latn], out=latn[:, c, sl], in0=lat[:, c, sl], in1=rs[:, sl], op=ALU.mult)

    def rope_tail(tc, src, Pn, s):
        sl = slice(tc * 512, (tc + 1) * 512)
        h = Pn // 2
        xst = xs[tc]
        P.g("dve", "tensor_copy", [src], [xst], out=xst[0:h, :], in_=src[h:Pn, :])
        P.g("dve", "tensor_copy", [src], [xst], out=xst[h:Pn, :], in_=src[0:h, :])
        P.g("dve", "tensor_tensor", [src, tcos], [t1[tc]], out=t1[tc][0:Pn, :], in0=src[0:Pn, :], in1=tcos[0:Pn, sl], op=ALU.mult)
        P.g("dve", "tensor_tensor", [xst, tsin], [t2[tc]], out=t2[tc][0:Pn, :], in0=xst[0:Pn, :], in1=tsin[0:Pn, sl], op=ALU.mult)
        P.g("dve", "tensor_tensor", [t1[tc], t2[tc]], [s], out=s[0:Pn, sl], in0=t1[tc][0:Pn, :], in1=t2[tc][0:Pn, :], op=ALU.add)

    def epi_rope64(dst_ap, scale):
        s = next_stg()

        def epi(tc, ps):
            P.g("dve", "tensor_scalar", [ps], [xf[tc]], out=xf[tc][0:64, :], in0=ps[0:64, :], scalar1=float(scale), scalar2=None, op0=ALU.mult)
            rope_tail(tc, xf[tc], 64, s)
            if tc == 1:
                P.dma("sp", dst_ap, s[0:64, :], [s], [])
        return epi

    def epi_headnorm_rope(dst_ap, scale, gt):
        s = next_stg()

        def epi(tc, ps):
            sqt = sq[tc]
            P.g("dve", "tensor_scalar", [ps], [t1[tc]], out=t1[tc][:, :], in0=ps[:, :], scalar1=float(scale), scalar2=None, op0=ALU.mult)
            P.g("act", "activation", [t1[tc]], [sqt], out=sqt[:, :], in_=t1[tc][:, :], func=AF.Square, scale=float(1.0 / scale))
            P.mm(sp_[tc][:, :], ones[:, :], sqt[:, :], True, True, [ones, sqt], [sp_[tc]])
            sl = slice(tc * 512, (tc + 1) * 512)
            P.g("act", "activation", [sp_[tc], epst], [rs], out=rs[:, sl], in_=sp_[tc][:, :], func=AF.Sqrt, bias=epst[:], scale=1.0 / 128)
            P.g("dve", "reciprocal", [rs], [rs], out=rs[:, sl], in_=rs[:, sl])
            P.g("dve", "tensor_scalar", [t1[tc], gt], [t1[tc]], out=t1[tc][:, :], in0=t1[tc][:, :], scalar1=gt[:, 0:1], scalar2=None, op0=ALU.mult)
            P.g("dve", "tensor_tensor", [t1[tc], rs], [xf[tc]], out=xf[tc][:, :], in0=t1[tc][:, :], in1=rs[:, sl], op=ALU.mult)
            rope_tail(tc, xf[tc], 128, s)
            if tc == 1:
                P.dma("sp", dst_ap, s[:, :], [s], [])
        return epi

    if stage == 0 or stage == -2:
        proj(T_G[0], 128, epi_silu(G[0, :, :]))
        P.emit(); P.close(); return nc
    import os as _os
    if not _os.environ.get("KSKIP_TCOS"):
        P.dma("sp", tcos[0:64, :], cosA[:, :], [], [tcos])
        P.dma("sp", tsin[0:64, :], sinA[:, :], [], [tsin])
    if not _os.environ.get("KSKIP_WUP"):
        P.dma("pool", wup[:, :], wuq[:, :], [], [wup])
    for c in range(int(_os.environ.get("KN_LAT", "6"))):
        proj(T_CQ[0] + c, 128, epi_latent(c, 6, 768))
    if stage == 10:
        P.emit(); P.close(); return nc
    latent_norm(6, 768, gqa_t)
    sA = 192 ** -0.5
    if stage == 11:
        P.emit(); P.close(); return nc
    for h in range(8):
        upproj(latn, 6, 1536, h * 192, 128, epi_plain(QA[h, 0:128, :], sA))
        if stage == 12:
            P.emit(); P.close(); return nc
        upproj(latn, 6, 1536, h * 192 + 128, 64, epi_rope64(QA[h, 128:192, :], sA))
        if stage == 13:
            P.emit(); P.close(); return nc
    if stage == 1:
        P.emit(); P.close(); return nc
    P.dma("pool", wup[:, 0:4 * 2048], wukv[:, :], [], [wup])
    for c in range(4):
        proj(T_CKV[0] + c, 128, epi_latent(c, 4, 512))
    latent_norm(4, 512, gkva_t)
    for h in range(8):
        upproj(latn, 4, 2048, h * 256, 128, epi_plain(KA[h, :, :], 1.0))
        upproj(latn, 4, 2048, h * 256 + 128, 128, epi_v(VA, h * 128))
    proj(T_KPE, 64, epi_rope64(KPE[:, :], 1.0))
    if stage == 2:
        P.emit(); P.close(); return nc
    sB = 128 ** -0.5
    for i in range(12):
        proj(T_BQ[0] + i, 128, epi_plain(QB[i, :, :], sB))
    for i in range(12):
        proj(T_BK[0] + i, 128, epi_plain(KB[i, :, :], 1.0))
    for i in range(12):
        proj(T_BV[0] + i, 128, epi_v(VB, i * 128))
    if stage == 3:
        P.emit(); P.close(); return nc
    P.dma("sp", tcos[:, :], cosC[:, :], [], [tcos])
    P.dma("sp", tsin[:, :], sinC[:, :], [], [tsin])
    for i in range(8):
        proj(T_CQh[0] + i, 128, epi_headnorm_rope(QC[i, :, :], sB, gqn_t))
    for i in range(2):
        proj(T_CK[0] + i, 128, epi_headnorm_rope(KC[i, :, :], 1.0, gkn_t))
    for i in range(2):
        proj(T_CV[0] + i, 128, epi_v(VC, i * 128))
    if stage == 4:
        P.emit(); P.close(); return nc
    for i in range(8):
        proj(T_DQ[0] + i, 128, epi_plain(QD[i, :, :], sB))
    for i in range(2):
        proj(T_DK[0] + i, 128, epi_plain(KD[i, :, :], 1.0))
    for i in range(2):
        proj(T_DV[0] + i, 128, epi_v(VD, i * 128))
    for i in range(36):
        proj(T_G[0] + i, 128, epi_silu(G[i, :, :]))

    P.emit()
    P.close()
    return nc


DILS = (1, 4, 16)
NKB = tuple(1024 + 128 * d for d in DILS)


def build_T1():
    nc = bass.Bass("TRN2", target_bir_lowering=False)
    P = Prog(nc)
    IN = "ExternalInput"
    QA = P.dram("QA", [8, 192, S_LOC], BF16, IN)
    QB = P.dram("QB", [12, 128, S_LOC], BF16, IN)
    QC = P.dram("QC", [8, 128, S_LOC], BF16, IN)
    QD = P.dram("QD", [8, 128, S_LOC], BF16, IN)
    G = P.dram("G", [36, 128, S_LOC], BF16, IN)
    KAall = P.dram("KAall", [8, 128, 4096], BF16, IN)
    KPEall = P.dram("KPEall", [64, 4096], BF16, IN)
    VAall = P.dram("VAall", [4096, 1024], BF16, IN)
    KCall = P.dram("KCall", [2, 128, 4096], BF16, IN)
    VCall = P.dram("VCall", [4096, 256], BF16, IN)
    KDh = P.dram("KDh", [2, 128, 1280], BF16, IN)
    VDh = P.dram("VDh", [1280, 256], BF16, IN)
    KBh = [P.dram(f"KBh{p}", [4, 128, NKB[p]], BF16, IN) for p in range(3)]
    VBh = [P.dram(f"VBh{p}", [NKB[p], 512], BF16, IN) for p in range(3)]
    MDb = P.dram("MDb", [128, 2 * 3 * 512], F32, IN)
    MDm = P.dram("MDm", [128, 2 * 3 * 512], F32, IN)
    MBb = P.dram("MBb", [64, 12 * 192], F32, IN)
    MBm = P.dram("MBm", [64, 12 * 192], F32, IN)
    onesL_d = P.dram("onesL", [128, 128], F32, IN)
    onesR_d = P.dram("onesR", [128, 128], F32, IN)
    sinks_d = P.dram("sinksb", [128, 8], F32, IN)
    MT = P.dram("MT", [36, 128, S_LOC], BF16, "ExternalOutput")

    q0 = [P.sbuf(f"q0{i}", [128, S_LOC], BF16) for i in range(2)]
    q1 = [P.sbuf(f"q1{i}", [128, S_LOC], BF16) for i in range(2)]
    kn = [P.sbuf(f"kn{i}", [128, 4096], BF16) for i in range(2)]
    kpe = P.sbuf("kpe", [128, 4096], BF16)
    vv = [P.sbuf(f"vv{i}", [128, 32, 128], BF16) for i in range(2)]
    pts = [P.sbuf(f"pt{i}", [128, 512], BF16) for i in range(3)]
    gts = [P.sbuf(f"gt{i}", [128, S_LOC], BF16) for i in range(2)]
    rinv = [P.sbuf(f"rinv{i}", [128, 512], F32) for i in range(2)]
    otmp = [P.sbuf(f"otmp{i}", [128, 512], F32) for i in range(2)]
    ost = [P.sbuf(f"ost{i}", [128, S_LOC], BF16) for i in range(2)]
    ones = P.sbuf("ones", [128, 128], BF16)
    onesL = P.sbuf("onesLs", [128, 128], BF16)
    onesR = P.sbuf("onesRs", [128, 128], BF16)
    esk = P.sbuf("esk", [128, 8], F32)
    MD = P.sbuf("MD", [128, 2 * 3 * 512], F32)
    MD2 = P.sbuf("MD2", [128, 2 * 3 * 512], F32)
    MB = P.sbuf("MB", [128, 12 * 192], F32)
    MB2 = P.sbuf("MB2", [128, 12 * 192], F32)
    qd4 = P.sbuf("qd4", [128, 4, S_LOC], BF16)
    gt4 = P.sbuf("gt4", [128, 4, S_LOC], BF16)
    ost4 = P.sbuf("ost4", [128, 4, S_LOC], BF16)
    kdh = P.sbuf("kdh", [128, 1280], BF16)
    vdh = P.sbuf("vdh", [128, 10, 128], BF16)
    tmpd = [P.sbuf(f"tmpd{i}", [128, 512], F32) for i in range(3)]
    ptd = [P.sbuf(f"ptd{i}", [128, 512], BF16) for i in range(3)]
    kb_t = P.sbuf("kb_t", [128, 3072], BF16)
    vb_t = P.sbuf("vb_t", [128, 48, 128], BF16)
    qb_t = P.sbuf("qb_t", [128, S_LOC], BF16)
    num = [P.sbuf(f"num{i}", [128, S_LOC], F32) for i in range(3)]
    Ls = [P.sbuf(f"Ls{i}", [128, S_LOC], F32) for i in range(3)]
    ltot = P.sbuf("ltot", [128, S_LOC], F32)
    tmpb = P.sbuf("tmpb", [128, 192], F32)
    ptb = P.sbuf("ptb", [128, 192], BF16)

    sts = [P.psum(f"st{i}", [128, 512], F32) for i in range(3)]
    Ops = [P.psum(f"Op{i}", [128, 512], F32) for i in range(2)]
    Lps = [P.psum(f"Lp{i}", [128, 512], F32) for i in range(2)]

    P.g("pool", "memset", [], [ones], ones[:], 1.0)
    P.dma("pool", onesL[:], onesL_d[:], [], [onesL])
    P.dma("pool", onesR[:], onesR_d[:], [], [onesR])
    P.dma("sp", esk[:], sinks_d[:], [], [esk])
    P.g("act", "activation", [esk], [esk], out=esk[:], in_=esk[:], func=AF.Exp)
    P.dma("sp", MD[:], MDb[:], [], [MD])
    P.dma("sp", MD2[:], MDm[:], [], [MD2])
    P.g("dve", "tensor_tensor", [MD, MD2], [MD], out=MD[:], in0=MD[:], in1=MD2[:], op=ALU.add)
    P.dma("sp", MB[0:64, :], MBb[:], [], [MB])
    P.dma("sp", MB2[0:64, :], MBm[:], [], [MB2])
    P.g("dve", "tensor_tensor", [MB, MB2], [MB], out=MB[0:64, :], in0=MB[0:64, :], in1=MB2[0:64, :], op=ALU.add)

    st = {"h": 0, "ol": 0, "fin": 0}

    def finalize(Ot, Lt, qc, gt, os_, esk_cols=None):
        k = st["fin"] % 2
        st["fin"] += 1
        ri = rinv[k]
        if esk_cols is None:
            P.g("dve", "reciprocal", [Lt], [ri], out=ri[:, :], in_=Lt[:, :])
        else:
            for i, hc in enumerate(esk_cols):
                P.g("dve", "tensor_scalar", [Lt, esk], [ri], out=ri[:, i * 128:(i + 1) * 128], in0=Lt[:, i * 128:(i + 1) * 128],
                    scalar1=esk[:, hc:hc + 1], scalar2=None, op0=ALU.add)
            P.g("dve", "reciprocal", [ri], [ri], out=ri[:, :], in_=ri[:, :])
        P.g("dve", "tensor_tensor", [Ot, ri], [otmp[k]], out=otmp[k][:, :], in0=Ot[:, :], in1=ri[:, :], op=ALU.mult)
        return otmp[k]

    def dense_head(qparts, kparts, vt, gt, mt_idx):
        os_ = ost[st["h"] % 2]
        st["h"] += 1
        for qc in range(2):
            Ot = Ops[st["ol"] % 2]
            Lt = Lps[st["ol"] % 2]
            st["ol"] += 1

            def qk(kb):
                s_ = sts[kb % 2]
                n = len(qparts)
                for i in range(n):
                    qt, kp = qparts[i]
                    kt = kparts[i]
                    P.mm(s_[:, :], kt[0:kp, kb * 128:(kb + 1) * 128], qt[0:kp, qc * 512:(qc + 1) * 512], i == 0, i == n - 1,
                         [kt, qt], [s_])
            qk(0)
            for kb in range(32):
                if kb + 1 < 32:
                    qk(kb + 1)
                s_ = sts[kb % 2]
                pt = pts[kb % 3]
                P.g("act", "activation", [s_], [pt], out=pt[:, :], in_=s_[:, :], func=AF.Exp)
                P.mm(Ot[:, :], vt[:, kb, :], pt[:, :], kb == 0, kb == 31, [vt, pt], [Ot])
                P.mm(Lt[:, :], ones[:, :], pt[:, :], kb == 0, kb == 31, [ones, pt], [Lt])
            o = finalize(Ot, Lt, qc, gt, os_)
            P.g("dve", "tensor_tensor", [o, gt], [os_], out=os_[:, qc * 512:(qc + 1) * 512], in0=o[:, :],
                in1=gt[:, qc * 512:(qc + 1) * 512], op=ALU.mult)
        P.dma("sp", MT[mt_idx, :, :], os_[:, :], [os_], [])

    P.dma("sp", kpe[0:64, :], KPEall[:, :], [], [kpe])
    for h in range(8):
        b = h % 2
        P.dma("sp", q0[b][:, :], QA[h, 0:128, :], [], [q0[b]])
        P.dma("sp", q1[b][0:64, :], QA[h, 128:192, :], [], [q1[b]])
        P.dma("sp", kn[b][:, :], KAall[h, :, :], [], [kn[b]])
        P.dma("sp", vv[b][:, :, :], VAall[:, h * 128:(h + 1) * 128].rearrange("(b p) d -> p b d", p=128), [], [vv[b]])
        P.dma("sp", gts[b][:, :], G[h, :, :], [], [gts[b]])
        dense_head([(q0[b], 128), (q1[b], 64)], [kn[b], kpe], vv[b], gts[b], h)
    for g in range(2):
        P.dma("sp", kn[g][:, :], KCall[g, :, :], [], [kn[g]])
        P.dma("sp", vv[g][:, :, :], VCall[:, g * 128:(g + 1) * 128].rearrange("(b p) d -> p b d", p=128), [], [vv[g]])
        for i in range(4):
            hq = g * 4 + i
            b = hq % 2
            P.dma("sp", q0[b][:, :], QC[hq, :, :], [], [q0[b]])
            P.dma("sp", gts[b][:, :], G[20 + hq, :, :], [], [gts[b]])
            dense_head([(q0[b], 128)], [kn[g]], vv[g], gts[b], 20 + hq)
    for g in range(2):
        for i in range(4):
            P.dma("sp", qd4[:, i, :], QD[g * 4 + i, :, :], [], [qd4])
            P.dma("sp", gt4[:, i, :], G[28 + g * 4 + i, :, :], [], [gt4])
        P.dma("sp", kdh[:, :], KDh[g, :, :], [], [kdh])
        P.dma("sp", vdh[:, :, :], VDh[:, g * 128:(g + 1) * 128].rearrange("(b p) d -> p b d", p=128), [], [vdh])
        for qb in range(8):
            Ot = Ops[st["ol"] % 2]
            Lt = Lps[st["ol"] % 2]
            st["ol"] += 1
            qsl = slice(qb * 128, (qb + 1) * 128)
            for j in range(3):
                P.mm(sts[j][:, :], kdh[:, (qb + j) * 128:(qb + j + 1) * 128], qd4[:, :, qsl], True, True, [kdh, qd4], [sts[j]])
                off = (g * 3 + j) * 512
                P.g("dve", "tensor_tensor", [sts[j], MD], [tmpd[j]], out=tmpd[j][:, :], in0=sts[j][:, :], in1=MD[:, off:off + 512], op=ALU.add)
                P.g("act", "activation", [tmpd[j]], [ptd[j]], out=ptd[j][:, :], in_=tmpd[j][:, :], func=AF.Exp)
            for j in range(3):
                kbi = qb + j
                on = onesL if kbi == 0 else (onesR if kbi == 9 else ones)
                P.mm(Ot[:, :], vdh[:, kbi, :], ptd[j][:, :], j == 0, j == 2, [vdh, ptd[j]], [Ot])
                P.mm(Lt[:, :], on[:, :], ptd[j][:, :], j == 0, j == 2, [on, ptd[j]], [Lt])
            o = finalize(Ot, Lt, 0, None, None, esk_cols=[g * 4 + i for i in range(4)])
            P.g("dve", "tensor_tensor", [o, gt4], [ost4], out=ost4[:, :, qsl], in0=o[:, :].rearrange("p (a b) -> p a b", a=4),
                in1=gt4[:, :, qsl], op=ALU.mult)
        for i in range(4):
            P.dma("sp", MT[28 + g * 4 + i, :, :], ost4[:, i, :], [ost4], [])
    stb = sts[0]
    for hh in range(4):
        for p in range(3):
            dil = DILS[p]
            nq = 1024 // dil
            nqc = nq // 64
            nkc = nqc + 2
            hidx = p * 4 + hh
            P.dma("sp", qb_t[:, :], QB[hidx, :, :], [], [qb_t])
            P.dma("sp", kb_t[:, 0:NKB[p]], KBh[p][hh, :, :], [], [kb_t])
            P.dma("sp", vb_t[0:64, 0:NKB[p] // 64, :], VBh[p][:, hh * 128:(hh + 1) * 128].rearrange("(c k) d -> k c d", k=64), [], [vb_t])
            qi = 0
            bank = []
            for r in range(dil):
                for i in range(nqc):
                    if qi % 8 == 0:
                        Ot = Ops[st["ol"] % 2]
                        Lt = Lps[st["ol"] % 2]
                        st["ol"] += 1
                        bank = []
                    s0 = r + dil * 64 * i
                    qsl = slice(s0, s0 + dil * 63 + 1, dil)
                    kc0 = r * nkc + i
                    for j in range(3):
                        P.mm(stb[0:64, j * 64:(j + 1) * 64], kb_t[:, (kc0 + j) * 64:(kc0 + j + 1) * 64], qb_t[:, qsl], True, True,
                             [kb_t, qb_t], [stb])
                    P.g("dve", "tensor_tensor", [stb, MB], [tmpb], out=tmpb[0:64, :], in0=stb[0:64, 0:192],
                        in1=MB[0:64, hidx * 192:(hidx + 1) * 192], op=ALU.add)
                    P.g("act", "activation", [tmpb], [ptb], out=ptb[0:64, :], in_=tmpb[0:64, :], func=AF.Exp)
                    col = (qi % 8) * 64
                    for j in range(3):
                        on = onesL if (i + j == 0) else (onesR if (i + j == nkc - 1) else ones)
                        P.mm(Ot[:, col:col + 64], vb_t[0:64, kc0 + j, :], ptb[0:64, j * 64:(j + 1) * 64], j == 0, j == 2, [vb_t, ptb], [Ot])
                        P.mm(Lt[:, col:col + 64], on[0:64, :], ptb[0:64, j * 64:(j + 1) * 64], j == 0, j == 2, [on, ptb], [Lt])
                    bank.append((col, qsl))
                    qi += 1
                    if qi % 8 == 0:
                        for (cc, qs) in bank:
                            P.g("dve", "tensor_copy", [Ot], [num[p]], out=num[p][:, qs], in_=Ot[:, cc:cc + 64])
                            P.g("dve", "tensor_copy", [Lt], [Ls[p]], out=Ls[p][:, qs], in_=Lt[:, cc:cc + 64])
        P.g("dve", "tensor_tensor", [Ls[0], Ls[1]], [ltot], out=ltot[:, :], in0=Ls[0][:, :], in1=Ls[1][:, :], op=ALU.add)
        P.g("dve", "tensor_tensor", [ltot, Ls[2]], [ltot], out=ltot[:, :], in0=ltot[:, :], in1=Ls[2][:, :], op=ALU.add)
        P.g("dve", "reciprocal", [ltot], [ltot], out=ltot[:, :], in_=ltot[:, :])
        for p in range(3):
            b = p % 2
            P.dma("sp", gts[b][:, :], G[8 + p * 4 + hh, :, :], [], [gts[b]])
            P.g("dve", "tensor_tensor", [num[p], ltot], [num[p]], out=num[p][:, :], in0=num[p][:, :], in1=ltot[:, :], op=ALU.mult)
            os_ = ost[b]
            P.g("dve", "tensor_tensor", [num[p], gts[b]], [os_], out=os_[:, :], in0=num[p][:, :], in1=gts[b][:, :], op=ALU.mult)
            P.dma("sp", MT[8 + p * 4 + hh, :, :], os_[:, :], [os_], [])
    P.emit()
    P.close()
    return nc


def build_T2(final):
    nc = bass.Bass("TRN2", target_bir_lowering=False)
    P = Prog(nc)
    IN = "ExternalInput"
    MTi = P.dram("MTi", [36, 128, S_LOC], BF16, IN)
    x = P.dram("x", [S_LOC, 4096], F32, IN)
    wo = P.dram("wo", [16, 128, 36 * 256], F32, IN)
    gfin = P.dram("gfin", [128, 4096], F32, IN)
    xo = P.dram("xo", [S_LOC, 4096], F32, "ExternalOutput")
    xsc = P.dram("xsc", [S_LOC, 4096], F32) if final else None

    mt = P.sbuf("mt", [128, 36, S_LOC], BF16)
    wob = [P.sbuf(f"wob{i}", [128, 36, 256], BF16) for i in range(2)]
    xt = [P.sbuf(f"xt{i}", [128, 256], F32) for i in range(3)]
    yt = [P.sbuf(f"yt{i}", [128, 256], F32) for i in range(3)]
    ps = [P.psum(f"ps{i}", [128, 512], F32) for i in range(4)]
    mts = [P.sub(mt, f"mt{c}") for c in range(36)]
    for c in range(36):
        P.dma("sp", mt[:, c, :], MTi[c, :, :], [], [mts[c]])
    n = 0
    dst = xsc if final else xo
    dsts = [P.sub(dst, f"dst{tt}") for tt in range(8)] if final else None
    for dc in range(16):
        wb = wob[dc % 2]
        P.dma("pool", wb[:, :, :], wo[dc].rearrange("p (c n) -> p c n", c=36), [], [wb])
        for tt in range(8):
            pt = ps[n % 4]
            xt_ = xt[n % 3]
            yt_ = yt[n % 3]
            n += 1
            P.dma("sp", xt_[:, :], x[tt * 128:(tt + 1) * 128, dc * 256:(dc + 1) * 256], [], [xt_])
            for c in range(36):
                P.mm(pt[:, 0:256], mt[:, c, tt * 128:(tt + 1) * 128], wb[:, c, :], c == 0, c == 35, [mts[c], wb], [pt])
            P.g("dve", "tensor_tensor", [pt, xt_], [yt_], out=yt_[:, :], in0=pt[:, 0:256], in1=xt_[:, :], op=ALU.add)
            P.dma("sp", dst[tt * 128:(tt + 1) * 128, dc * 256:(dc + 1) * 256], yt_[:, :], [yt_], [dsts[tt]] if final else [])
    if final:
        xf = P.sbuf("xf", [128, 4096], F32)
        gf = P.sbuf("gf", [128, 4096], F32)
        of = P.sbuf("of", [128, 4096], F32)
        junk = P.sbuf("junk", [128, 512], F32)
        ssq8 = P.sbuf("ssq8", [128, 8], F32)
        ssq = P.sbuf("ssq", [128, 1], F32)
        epst = P.sbuf("epst", [128, 1], F32)
        P.g("pool", "memset", [], [epst], epst[:], 1e-6)
        P.dma("sp", gf[:, :], gfin[:, :], [], [gf])
        for tt in range(8):
            P.dma("sp", xf[:, :], xsc[tt * 128:(tt + 1) * 128, :], [dsts[tt]], [xf])
            P.g("dve", "memset", [], [ssq8], ssq8[:], 0.0)
            for k in range(8):
                P.g("act", "activation", [xf, ssq8], [junk, ssq8], out=junk[:, :], in_=xf[:, k * 512:(k + 1) * 512], func=AF.Square,
                    accum_out=ssq8[:, k:k + 1])
            P.g("dve", "reduce_sum", [ssq8], [ssq], out=ssq[:, 0:1], in_=ssq8[:, 0:8], axis=AX.X)
            P.g("act", "activation", [ssq, epst], [ssq], out=ssq[:], in_=ssq[:], func=AF.Sqrt, bias=epst[:], scale=1.0 / 4096)
            P.g("dve", "reciprocal", [ssq], [ssq], out=ssq[:], in_=ssq[:])
            P.g("dve", "tensor_scalar", [xf, ssq], [xf], out=xf[:, :], in0=xf[:, :], scalar1=ssq[:, 0:1], scalar2=None, op0=ALU.mult)
            P.g("dve", "tensor_tensor", [xf, gf], [of], out=of[:, :], in0=xf[:, :], in1=gf[:, :], op=ALU.mult)
            P.dma("sp", xo[tt * 128:(tt + 1) * 128, :], of[:, :], [of], [])
    P.emit()
    P.close()
    return nc


BF = ml_dtypes.bfloat16
S = 4096
PERM_C = np.concatenate([np.arange(0, 32), np.arange(64, 96), np.arange(32, 64), np.arange(96, 128)])


def rope_angles(pos, dim):
    inv = (10000.0 ** (-np.arange(0, dim, 2, dtype=np.float32) / dim)).astype(np.float32)
    return pos.astype(np.float32)[:, None] * inv[None, :]


def rope_tables(j):
    pos = np.arange(j * S_LOC, (j + 1) * S_LOC)
    at = rope_angles(pos, 64)
    ar = rope_angles(pos // 64, 64)
    ac = rope_angles(pos % 64, 64)
    cosA = np.concatenate([np.cos(at), np.cos(at)], 1).T
    sinA = np.concatenate([-np.sin(at), np.sin(at)], 1).T
    cosC = np.concatenate([np.cos(ar), np.cos(ac), np.cos(ar), np.cos(ac)], 1).T
    sinC = np.concatenate([-np.sin(ar), -np.sin(ac), np.sin(ar), np.sin(ac)], 1).T
    f = lambda a: np.ascontiguousarray(a, dtype=np.float32)
    return f(cosA), f(sinA), f(cosC), f(sinC)


def layer_weights_P(inp, l):
    w_in = inp["w_in"][l]
    cols = []
    for i in range(10):
        cols.append(np.arange(i * 128, (i + 1) * 128))
    tiles = np.zeros((107, 128, 32, 128), np.float32)
    w3 = w_in.reshape(32, 128, 13632)
    for i in range(10):
        tiles[i] = w3[:, :, i * 128:(i + 1) * 128].transpose(1, 0, 2)
    tiles[10, :, :, 0:64] = w3[:, :, 1280:1344].transpose(1, 0, 2)
    for t in range(11, 107):
        c0 = 1344 + (t - 11) * 128
        blk = w3[:, :, c0:c0 + 128].transpose(1, 0, 2)
        if 47 <= t < 57:
            blk = blk[:, :, PERM_C]
        tiles[t] = blk
    d = {}
    d["wt"] = tiles.reshape(107, 128, 4096)
    d["gattn"] = np.ascontiguousarray(inp["g_attn"][l].reshape(32, 128).T)
    d["gqa"] = np.ascontiguousarray(inp["g_qa"][l].reshape(6, 128).T)
    d["gkva"] = np.ascontiguousarray(inp["g_kva"][l].reshape(4, 128).T)
    d["gqn"] = np.ascontiguousarray(inp["g_qn"][l][PERM_C].reshape(128, 1))
    d["gkn"] = np.ascontiguousarray(inp["g_kn"][l][PERM_C].reshape(128, 1))
    d["wuq"] = np.ascontiguousarray(inp["w_uq"][l].reshape(6, 128, 1536).transpose(1, 0, 2).reshape(128, 6 * 1536))
    d["wukv"] = np.ascontiguousarray(inp["w_ukv"][l].reshape(4, 128, 2048).transpose(1, 0, 2).reshape(128, 4 * 2048))
    return d


def in_maps_P(inp, l, xcur):
    wd = layer_weights_P(inp, l)
    maps = []
    for c in range(8):
        b, j = divmod(c, 4)
        m = dict(wd)
        m["x"] = np.ascontiguousarray(xcur[b, j * S_LOC:(j + 1) * S_LOC, :])
        m["cosA"], m["sinA"], m["cosC"], m["sinC"] = rope_tables(j)
        maps.append(m)
    return maps


def rel_bucket_np(rel):
    nb = 16
    me = 8
    ret = np.where(rel > 0, nb, 0)
    n = np.abs(rel)
    nf = np.maximum(n, 1).astype(np.float32)
    large = me + (np.log(nf / np.float32(me)) / np.float32(math.log(1024 / me)) * np.float32(nb - me)).astype(np.int32)
    large = np.minimum(large, nb - 1)
    return ret + np.where(n < me, n, large)


def bias_tables(rel_bias):
    k = np.arange(128)[:, None]
    q = np.arange(128)[None, :]
    MDb = np.zeros((128, 2, 3, 4, 128), np.float32)
    MDm = np.zeros((128, 2, 3, 4, 128), np.float32)
    for jj in range(3):
        rel = (jj - 1) * 128 + k - q
        bk = rel_bucket_np(rel)
        ok = np.abs(rel) <= 128
        for g in range(2):
            for i in range(4):
                MDb[:, g, jj, i, :] = np.where(ok, rel_bias[bk, 12 + g * 4 + i], 0.0)
                MDm[:, g, jj, i, :] = np.where(ok, 0.0, -30000.0)
    k = np.arange(64)[:, None]
    q = np.arange(64)[None, :]
    MBb = np.zeros((64, 12, 3, 64), np.float32)
    MBm = np.zeros((64, 12, 3, 64), np.float32)
    for p, dil in enumerate((1, 4, 16)):
        for jj in range(3):
            rel = (jj - 1) * 64 + k - q
            bk = rel_bucket_np(dil * rel)
            ok = np.abs(rel) <= 64
            for hh in range(4):
                MBb[:, p * 4 + hh, jj, :] = np.where(ok, rel_bias[bk, p * 4 + hh], 0.0)
                MBm[:, p * 4 + hh, jj, :] = np.where(ok, 0.0, -30000.0)
    return (MDb.reshape(128, -1), MDm.reshape(128, -1), MBb.reshape(64, -1), MBm.reshape(64, -1))


def in_maps_T1(inp, l, R):
    MDb, MDm, MBb, MBm = bias_tables(inp["rel_bias"])
    sinksb = np.ascontiguousarray(np.broadcast_to(inp["sinks"][l][None, :], (128, 8)), dtype=np.float32)
    maps = []
    for b in range(2):
        rs = [R[4 * b + j] for j in range(4)]
        cat = lambda name, ax: np.concatenate([r[name] for r in rs], axis=ax)
        KAall = cat("KA", 2)
        KPEall = cat("KPE", 1)
        VAall = cat("VA", 0)
        KCall = cat("KC", 2)
        VCall = cat("VC", 0)
        KD = np.pad(cat("KD", 2), ((0, 0), (0, 0), (128, 128)))
        VD = np.pad(cat("VD", 0), ((128, 128), (0, 0)))
        KB = cat("KB", 2)
        VB = cat("VB", 0)
        KBp, VBp = [], []
        for p, d in enumerate((1, 4, 16)):
            kk = KB[p * 4:(p + 1) * 4].reshape(4, 128, 4096 // d, d).transpose(0, 1, 3, 2)
            kk = np.pad(kk, ((0, 0), (0, 0), (0, 0), (64, 64)))
            v_ = VB[:, p * 512:(p + 1) * 512].reshape(4096 // d, d, 512).transpose(1, 0, 2)
            v_ = np.pad(v_, ((0, 0), (64, 64), (0, 0)))
            KBp.append(kk)
            VBp.append(v_)
        for j in range(4):
            r = rs[j]
            m = {"QA": r["QA"], "QB": r["QB"], "QC": r["QC"], "QD": r["QD"], "G": r["G"],
                 "KAall": KAall, "KPEall": KPEall, "VAall": VAall, "KCall": KCall, "VCall": VCall,
                 "KDh": np.ascontiguousarray(KD[:, :, j * 1024:j * 1024 + 1280]),
                 "VDh": np.ascontiguousarray(VD[j * 1024:j * 1024 + 1280]),
                 "MDb": MDb, "MDm": MDm, "MBb": MBb, "MBm": MBm, "sinksb": sinksb,
                 "onesL": np.full((128, 128), 1.0 if j > 0 else 0.0, np.float32),
                 "onesR": np.full((128, 128), 1.0 if j < 3 else 0.0, np.float32)}
            for p, d in enumerate((1, 4, 16)):
                nq = 1024 // d
                m[f"KBh{p}"] = np.ascontiguousarray(KBp[p][:, :, :, j * nq:j * nq + nq + 128]).reshape(4, 128, d * (nq + 128))
                m[f"VBh{p}"] = np.ascontiguousarray(VBp[p][:, j * nq:j * nq + nq + 128, :]).reshape(d * (nq + 128), 512)
            maps.append(m)
    return maps


def in_maps_T2(inp, l, R1, xcur, final):
    wo = inp["w_out"][l].reshape(36, 128, 16, 256).transpose(2, 1, 0, 3).reshape(16, 128, 36 * 256)
    wo = np.ascontiguousarray(wo)
    gf = np.ascontiguousarray(np.broadcast_to(inp["g_final"][None, :], (128, 4096)), dtype=np.float32)
    maps = []
    for c in range(8):
        b, j = divmod(c, 4)
        maps.append({"MTi": R1[c]["MT"], "x": np.ascontiguousarray(xcur[b, j * 1024:(j + 1) * 1024, :]), "wo": wo, "gfin": gf})
    return maps


_PROGS = {}


def _prog(name):
    if name not in _PROGS:
        if name == "P":
            _PROGS[name] = build_P()
        elif name == "T1":
            _PROGS[name] = build_T1()
        elif name == "T2":
            _PROGS[name] = build_T2(False)
        else:
            _PROGS[name] = build_T2(True)
    return _PROGS[name]


def kernel(x, g_attn, w_in, g_qa, g_kva, w_uq, w_ukv, g_qn, g_kn, sinks, w_out, rel_bias, g_final):
    inp = dict(x=np.asarray(x, np.float32), g_attn=np.asarray(g_attn, np.float32), w_in=np.asarray(w_in, np.float32),
               g_qa=np.asarray(g_qa, np.float32), g_kva=np.asarray(g_kva, np.float32), w_uq=np.asarray(w_uq, np.float32),
               w_ukv=np.asarray(w_ukv, np.float32), g_qn=np.asarray(g_qn, np.float32), g_kn=np.asarray(g_kn, np.float32),
               sinks=np.asarray(sinks, np.float32), w_out=np.asarray(w_out, np.float32),
               rel_bias=np.asarray(rel_bias, np.float32), g_final=np.asarray(g_final, np.float32))
    xcur = inp["x"]
    cores = list(range(8))
    for l in range(2):
        R = run_bass_kernel_spmd(_prog("P"), in_maps_P(inp, l, xcur), core_ids=cores).results
        R1 = run_bass_kernel_spmd(_prog("T1"), in_maps_T1(inp, l, R), core_ids=cores).results
        final = (l == 1)
        R2 = run_bass_kernel_spmd(_prog("T2f" if final else "T2"), in_maps_T2(inp, l, R1, xcur, final), core_ids=cores).results
        xcur = np.stack([np.concatenate([np.asarray(R2[4 * b + j]["xo"], np.float32) for j in range(4)], 0) for b in range(2)], 0)
    return xcur
```

```python
import contextlib
import math
import numpy as np
import ml_dtypes
from concourse.bass_utils import run_bass_kernel_spmd
import concourse.bass as bass
import concourse.mybir as mybir

F32 = mybir.dt.float32
BF16 = mybir.dt.bfloat16
ALU = mybir.AluOpType
AF = mybir.ActivationFunctionType
AX = mybir.AxisListType

ENGS = ["pe", "act", "dve", "pool", "sp"]


class Tile:
    def __init__(self, name, t):
        self.name = name
        self.t = t
        self.writers = []
        self.readers = []
        self.dsem = None
        self.multi = False

    def __getitem__(self, idx):
        return self.t[idx]


class Op:
    __slots__ = ("eng", "fn", "waits", "signals", "sigval", "dma", "idx")


class Prog:
    def __init__(self, nc, same_engine_sync=True):
        self.nc = nc
        self.stack = contextlib.ExitStack()
        self.ops = {e: [] for e in ENGS}
        self.same_engine_sync = same_engine_sync
        self.dsems = []
        self.dsem_by_name = {}
        self.esem = {}
        self.ntile = 0
        self.arena = None
        self.arena_n32 = 0
        self.bump = 0
        self.banks = None
        self.nbank = 0
        self.live = []
        self.carry = []

    def init_arena(self, n32):
        self.arena = self.stack.enter_context(self.nc.sbuf_tensor("arena", [128, n32], F32))
        self.arena_n32 = n32
        self.banks = [self.stack.enter_context(self.nc.psum_tensor(f"bank{i}", [128, 512], F32)) for i in range(8)]

    def phase_reset(self):
        best = {}
        for tok in self.carry:
            key = (tok[0], tok[1])
            if key not in best or tok[2] > best[key][2]:
                best[key] = tok
        for t in self.live:
            for tok in t.writers + t.readers:
                key = (tok[0], tok[1])
                if key not in best or tok[2] > best[key][2]:
                    best[key] = tok
        self.carry = list(best.values())
        self.live = []
        self.bump = 0
        self.nbank = 0

    def _view(self, base_ap, shape, dt):
        ap = base_ap
        if dt != F32:
            ap = ap.bitcast(dt)
        if len(shape) == 3:
            ap = ap.rearrange("p (a b) -> p a b", a=shape[1])
        return ap

    def asb(self, name, shape, dt):
        esz = 4 if dt == F32 else 2
        nfree = 1
        for d in shape[1:]:
            nfree *= d
        n32 = (nfree * esz + 3) // 4
        n32 = (n32 + 7) // 8 * 8
        assert self.bump + n32 <= self.arena_n32, (name, self.bump, n32)
        ap = self._view(self.arena[:, self.bump:self.bump + (nfree * esz) // 4], shape, dt)
        self.bump += n32
        t = Tile(name, ap)
        t.writers = list(self.carry)
        self.live.append(t)
        return t

    def aps(self, name, shape, dt):
        b = self.banks[self.nbank]
        self.nbank += 1
        assert self.nbank <= 8
        t = Tile(name, self._view(b[:, :], shape, dt))
        t.writers = list(self.carry)
        self.live.append(t)
        return t

    def sem(self, name):
        return self.stack.enter_context(self.nc.semaphore(name))

    def sbuf(self, name, shape, dt):
        t = self.stack.enter_context(self.nc.sbuf_tensor(name, list(shape), dt))
        return Tile(name, t)

    def psum(self, name, shape, dt):
        t = self.stack.enter_context(self.nc.psum_tensor(name, list(shape), dt))
        return Tile(name, t)

    def dram(self, name, shape, dt, kind=None):
        if kind is None:
            t = self.nc.dram_tensor(name, list(shape), dt)
        else:
            t = self.nc.dram_tensor(name, list(shape), dt, kind=kind)
        tl = Tile(name, t)
        tl.multi = kind is None
        return tl

    def sub(self, tile, name=None):
        t = Tile(name or tile.name, tile.t)
        t.writers = list(tile.writers)
        t.readers = list(tile.readers)
        if tile in self.live:
            self.live.append(t)
        return t

    def _deps(self, eng, reads, writes):
        toks = []
        for t in reads:
            toks += t.writers
        for t in writes:
            if not t.multi:
                toks += t.writers
            toks += t.readers
        waits = []
        for tok in toks:
            if tok[0] == "E":
                if tok[1] == eng and (eng == "pe" or not self.same_engine_sync):
                    continue
                waits.append(tok)
            else:
                waits.append(("D", tok[1], self.dsems[tok[1]][1]))
        return waits

    def _commit(self, tok, reads, writes):
        for t in reads:
            t.readers.append(tok)
        for t in writes:
            if t.multi and not t.readers:
                t.writers = [w for w in t.writers if (w[0], w[1]) != (tok[0], tok[1])] + [tok]
            else:
                t.writers = [tok]
                t.readers = []

    def op(self, eng, fn, reads=(), writes=()):
        o = Op()
        o.eng = eng
        o.fn = fn
        o.dma = False
        o.signals = False
        o.sigval = None
        o.waits = self._deps(eng, reads, writes)
        o.idx = len(self.ops[eng])
        self.ops[eng].append(o)
        self._commit(("E", eng, o.idx), reads, writes)
        return o

    def dma(self, eng, out_ap, in_ap, reads=(), writes=(), semtile=None, **kw):
        if semtile is None:
            semtile = (list(writes) + list(reads))[0]
        if semtile.dsem is None:
            if semtile.name in self.dsem_by_name:
                semtile.dsem = self.dsem_by_name[semtile.name]
            else:
                semtile.dsem = len(self.dsems)
                self.dsems.append([self.sem("d_" + semtile.name + str(len(self.dsems))), 0])
                self.dsem_by_name[semtile.name] = semtile.dsem
        sid = semtile.dsem
        o = Op()
        o.eng = eng
        o.dma = True
        o.signals = False
        o.waits = self._deps(eng, reads, writes)
        self.dsems[sid][1] += 16
        val = self.dsems[sid][1]
        semh = self.dsems[sid][0]

        def fn(e, out_ap=out_ap, in_ap=in_ap, semh=semh, kw=kw):
            e.dma_start(out=out_ap, in_=in_ap, **kw).then_inc(semh, 16)

        o.fn = fn
        o.sigval = None
        o.idx = len(self.ops[eng])
        self.ops[eng].append(o)
        self._commit(("D", sid, val), reads, writes)
        return o

    def coll(self, name, in_ap, out_ap, groups, reads, writes):
        if name not in self.dsem_by_name:
            self.dsem_by_name[name] = len(self.dsems)
            self.dsems.append([self.sem("c_" + name), 0])
        sid = self.dsem_by_name[name]
        o = Op()
        o.eng = "pool"
        o.dma = True
        o.signals = False
        o.waits = self._deps("pool", reads, writes)
        self.dsems[sid][1] += 1
        val = self.dsems[sid][1]
        semh = self.dsems[sid][0]

        def fn(e):
            e.collective_compute("AllGather", ALU.bypass, replica_groups=groups, ins=[in_ap], outs=[out_ap]).then_inc(semh)

        o.fn = fn
        o.sigval = None
        o.idx = len(self.ops["pool"])
        self.ops["pool"].append(o)
        self._commit(("D", sid, val), reads, writes)
        return o

    def emit(self, final_waits_engine="sp"):
        nc = self.nc
        for e in ENGS:
            for o in self.ops[e]:
                for w in o.waits:
                    if w[0] == "E":
                        self.ops[w[1]][w[2]].signals = True
        for e in ENGS:
            c = 0
            for o in self.ops[e]:
                if o.signals:
                    c += 1
                    o.sigval = c
            self.esem[e] = self.sem("e_" + e)
        final = [(s[0], s[1]) for s in self.dsems if s[1] > 0]

        def replay(ename, eng):
            seen = {}
            for o in self.ops[ename]:
                for w in o.waits:
                    if w[0] == "E":
                        semh = self.esem[w[1]]
                        key = ("E", w[1])
                        val = self.ops[w[1]][w[2]].sigval
                    else:
                        semh = self.dsems[w[1]][0]
                        key = ("D", w[1])
                        val = w[2]
                    if seen.get(key, 0) >= val:
                        continue
                    seen[key] = val
                    eng.wait_ge(semh, val)
                if o.dma:
                    o.fn(eng)
                else:
                    ins = o.fn(eng)
                    if o.signals:
                        ins.then_inc(self.esem[ename], 1)
            if ename == final_waits_engine:
                for semh, val in final:
                    eng.wait_ge(semh, val)

        with nc.Block() as block:
            @block.tensor
            def _(e):
                replay("pe", e)

            @block.scalar
            def _(e):
                replay("act", e)

            @block.vector
            def _(e):
                replay("dve", e)

            @block.gpsimd
            def _(e):
                replay("pool", e)

            @block.sync
            def _(e):
                replay("sp", e)

    def close(self):
        self.stack.close()


    def g(self, eng, method, reads, writes, *args, **kw):
        return self.op(eng, lambda e: getattr(e, method)(*args, **kw), reads, writes)

    def mm(self, out, lhsT, rhs, start, stop, reads, writes):
        return self.op("pe", lambda e: e.matmul(out, lhsT=lhsT, rhs=rhs, start=start, stop=stop), reads, writes)

    def tr(self, out, in_, ident, reads, writes):
        return self.op("pe", lambda e: e.transpose(out, in_, ident), reads, writes)


S_LOC = 1024
NT = S_LOC // 128
NW = 107
EPS = 1e-6
T_CQ = (0, 6)
T_CKV = (6, 10)
T_KPE = 10
T_BQ = (11, 23)
T_BK = (23, 35)
T_BV = (35, 47)
T_CQh = (47, 55)
T_CK = (55, 57)
T_CV = (57, 59)
T_DQ = (59, 67)
T_DK = (67, 69)
T_DV = (69, 71)
T_G = (71, 107)
DILS = (1, 4, 16)
NKB = tuple(1024 + 128 * d for d in DILS)
KX_A, KX_B, KX_C, KX_D, KX_PE, KX_ROWS = 0, 1024, 2560, 2816, 3072, 3136
VX_A, VX_B, VX_C, VX_D, VX_COLS = 0, 1024, 2560, 2816, 3072
GROUPS = [[0, 1, 2, 3], [4, 5, 6, 7]]
ARENA_N32 = 46848


def kxall(D, r, row0, n):
    c = row0 // 256
    o = row0 % 256
    crows = 64 if c == 12 else 256
    return D["KXall"][c, r * crows + o:r * crows + o + n, :]


def vx(D, col0, n):
    return D["VX"][col0 // 256][:, col0 % 256:col0 % 256 + n]


def vxall(D, r, t0, t1, col0, n):
    return D["VXall"][col0 // 256][r * 1024 + t0:r * 1024 + t1, col0 % 256:col0 % 256 + n]


def phase_P(P, D, C, l):
    P.phase_reset()
    x = D["x"] if l == 0 else D["x1"]
    wt = D["wt"]
    QA, QB, QC, QD, G, KX, VX = D["QA"], D["QB"], D["QC"], D["QD"], D["G"], D["KX"], D["VX"]
    ident, ones, epst = C["ident"], C["ones"], C["epst"]

    hT = P.asb("hT", [128, 32, S_LOC], BF16)
    NB = 3
    ring = [P.asb(f"wr{i}", [128, 32, 128], BF16) for i in range(NB)]
    wup = P.asb("wup", [128, 6 * 1536], BF16)
    lat = P.asb("lat", [128, 6, S_LOC], F32)
    latn = P.asb("latn", [128, 6, S_LOC], BF16)
    tcos = P.asb("tcos", [128, S_LOC], F32)
    tsin = P.asb("tsin", [128, S_LOC], F32)
    sq = [P.asb(f"sq{i}", [128, 512], BF16) for i in range(2)]
    rs = P.asb("rs", [128, S_LOC], F32)
    xf = [P.asb(f"xf{i}", [128, 512], F32) for i in range(2)]
    xs = [P.asb("xs0", [128, 512], F32)] * 2
    t1 = [P.asb(f"t1{i}", [128, 512], F32) for i in range(2)]
    t2 = [P.asb("t20", [128, 512], F32)] * 2
    NSTG = 2
    stg = [P.asb(f"stg{i}", [128, S_LOC], BF16) for i in range(NSTG)]
    vst = [P.asb(f"vst{i}", [128, 8, 128], BF16) for i in range(2)]
    ssq = P.asb("ssq", [128, 2], F32)
    ssq8 = P.asb("ssq8", [128, 8], F32)
    junk = P.asb("junk", [128, 512], F32)
    rstd = P.asb("rstd", [128, 2], F32)
    gat = P.asb("gat", [128, 32], F32)
    gqa_t = P.asb("gqa_t", [128, 6], F32)
    gkva_t = P.asb("gkva_t", [128, 4], F32)
    gqn_t = P.asb("gqn_t", [128, 2], F32)
    gkn_t = P.asb("gkn_t", [128, 2], F32)
    pj = [P.aps(f"pj{i}", [128, 512], F32) for i in range(4)]
    tp = [P.aps(f"tp{i}", [128, 8, 128], BF16) for i in range(2)]
    sp_ = [P.aps(f"ssqp{i}", [128, 512], F32) for i in range(2)]

    P.dma("sp", gat[:, :], D["gattn"][l], [], [gat])
    P.dma("sp", gqa_t[:, :], D["gqa"][l], [], [gqa_t])
    P.dma("sp", gkva_t[:, :], D["gkva"][l], [], [gkva_t])
    P.dma("sp", gqn_t[:, 0:1], D["gqn"][l], [], [gqn_t])
    P.dma("sp", gkn_t[:, 0:1], D["gkn"][l], [], [gkn_t])

    hsub = [[P.sub(hT, f"hT{tt}a"), P.sub(hT, f"hT{tt}b")] for tt in range(NT)]
    xt_ap = lat[:, 0:4, :].rearrange("p a b -> p (a b)")
    xn_ap = latn[:, 0:4, :].rearrange("p a b -> p (a b)")
    ntp = 0
    xreads = [] if l == 0 else [x]
    for tt in range(NT):
        P.dma("sp", xt_ap, x[tt * 128:(tt + 1) * 128, :], xreads, [lat])
        P.g("dve", "memset", [], [ssq8], ssq8[:, :], 0.0)
        for k in range(8):
            P.g("act", "activation", [lat, ssq8], [junk, ssq8], out=junk[:, :], in_=xt_ap[:, k * 512:(k + 1) * 512], func=AF.Square,
                accum_out=ssq8[:, k:k + 1])
        P.g("dve", "reduce_sum", [ssq8], [ssq], out=ssq[:, 0:1], in_=ssq8[:, 0:8], axis=AX.X)
        P.g("act", "activation", [ssq, epst], [rstd], out=rstd[:, 0:1], in_=ssq[:, 0:1], func=AF.Sqrt, bias=epst[:], scale=1.0 / 4096)
        P.g("dve", "reciprocal", [rstd], [rstd], out=rstd[:, 0:1], in_=rstd[:, 0:1])
        P.g("dve", "tensor_scalar", [lat, rstd], [latn], out=xn_ap, in0=xt_ap, scalar1=rstd[:, 0:1], scalar2=None, op0=ALU.mult)
        for grp in range(4):
            tpt = tp[ntp % 2]
            ntp += 1
            for j in range(8):
                c = grp * 8 + j
                P.tr(tpt[:, j, :], xn_ap[:, c * 128:(c + 1) * 128], ident[:], [latn, ident], [tpt])
            for j in range(8):
                c = grp * 8 + j
                dst = hT[:, c, tt * 128:(tt + 1) * 128]
                P.g("dve", "tensor_scalar", [tpt, gat], [hsub[tt][j % 2]], out=dst, in0=tpt[:, j, :],
                    scalar1=gat[:, c:c + 1], scalar2=None, op0=ALU.mult)

    def hreads(tc):
        r = []
        for tt in range(tc * 4, tc * 4 + 4):
            r += hsub[tt]
        return r

    state = {"k": 0, "stg": 0, "pj": 0, "vst": 0, "tp": ntp}

    def proj(widx, M, epi):
        slot = ring[state["k"] % NB]
        state["k"] += 1
        P.dma("pool", slot[:, :, :], wt[l, widx].rearrange("p (c n) -> p c n", c=32), [], [slot])
        for tc in range(2):
            ps = pj[state["pj"] % 4]
            state["pj"] += 1
            hr = hreads(tc)
            for c in range(32):
                P.mm(ps[0:M, :], slot[:, c, 0:M], hT[:, c, tc * 512:(tc + 1) * 512], c == 0, c == 31, [slot] + hr, [ps])
            epi(tc, ps)

    def upproj(src, nchunk, wcols, col0, M, epi):
        for tc in range(2):
            ps = pj[state["pj"] % 4]
            state["pj"] += 1
            for c in range(nchunk):
                P.mm(ps[0:M, :], wup[:, c * wcols + col0: c * wcols + col0 + M], src[:, c, tc * 512:(tc + 1) * 512],
                     c == 0, c == nchunk - 1, [wup, src], [ps])
            epi(tc, ps)

    def next_stg():
        s = stg[state["stg"] % NSTG]
        state["stg"] += 1
        return s

    def scopy(out, in_, scale, reads, writes):
        P.g("dve", "tensor_scalar", reads, writes, out=out, in0=in_, scalar1=float(scale), scalar2=None, op0=ALU.mult)

    def epi_plain(dst_ap, dst_tile, scale, M=128):
        s = next_stg()

        def epi(tc, ps):
            scopy(s[0:M, tc * 512:(tc + 1) * 512], ps[0:M, :], scale, [ps], [s])
            if tc == 1:
                P.dma("sp", dst_ap, s[0:M, :], [s], [dst_tile])
        return epi

    def epi_silu(dst_ap, dst_tile):
        s = next_stg()

        def epi(tc, ps):
            P.g("act", "activation", [ps], [s], out=s[:, tc * 512:(tc + 1) * 512], in_=ps[:, :], func=AF.Silu)
            if tc == 1:
                P.dma("sp", dst_ap, s[:, :], [s], [dst_tile])
        return epi

    def epi_v(col0):
        s = next_stg()

        def epi(tc, ps):
            scopy(s[:, tc * 512:(tc + 1) * 512], ps[:, :], 1.0, [ps], [s])
            if tc == 1:
                tpt = tp[state["tp"] % 2]
                state["tp"] += 1
                for b in range(8):
                    P.tr(tpt[:, b, :], s[:, b * 128:(b + 1) * 128], ident[:], [s, ident], [tpt])
                vs = vst[state["vst"] % 2]
                state["vst"] += 1
                P.g("dve", "tensor_copy", [tpt], [vs], out=vs[:, :, :], in_=tpt[:, :, :])
                P.dma("sp", vx(D, col0, 128).rearrange("(b p) d -> p b d", p=128), vs[:, :, :], [vs], [VX])
        return epi

    def epi_latent(c, n, ndim):
        def epi(tc, ps):
            sl = slice(tc * 512, (tc + 1) * 512)
            P.g("dve", "tensor_copy", [ps], [lat], out=lat[:, c, sl], in_=ps[:, :])
            sqt = sq[tc]
            P.g("act", "activation", [lat], [sqt], out=sqt[:, :], in_=lat[:, c, sl], func=AF.Square)
            P.mm(sp_[tc][:, :], ones[:, :], sqt[:, :], c == 0, c == n - 1, [ones, sqt], [sp_[tc]])
        return epi

    def latent_norm(n, ndim, gt):
        for tc in range(2):
            sl = slice(tc * 512, (tc + 1) * 512)
            P.g("act", "activation", [sp_[tc], epst], [rs], out=rs[:, sl], in_=sp_[tc][:, :], func=AF.Sqrt, bias=epst[:],
                scale=1.0 / ndim)
            P.g("dve", "reciprocal", [rs], [rs], out=rs[:, sl], in_=rs[:, sl])
            for c in range(n):
                P.g("dve", "tensor_scalar", [lat, gt], [lat], out=lat[:, c, sl], in0=lat[:, c, sl], scalar1=gt[:, c:c + 1], scalar2=None, op0=ALU.mult)
                P.g("dve", "tensor_tensor", [lat, rs], [latn], out=latn[:, c, sl], in0=lat[:, c, sl], in1=rs[:, sl], op=ALU.mult)

    def rope_tail(tc, src, Pn, s):
        sl = slice(tc * 512, (tc + 1) * 512)
        h = Pn // 2
        xst = xs[tc]
        P.g("dve", "tensor_copy", [src], [xst], out=xst[0:h, :], in_=src[h:Pn, :])
        P.g("dve", "tensor_copy", [src], [xst], out=xst[h:Pn, :], in_=src[0:h, :])
        P.g("dve", "tensor_tensor", [src, tcos], [t1[tc]], out=t1[tc][0:Pn, :], in0=src[0:Pn, :], in1=tcos[0:Pn, sl], op=ALU.mult)
        P.g("dve", "tensor_tensor", [xst, tsin], [t2[tc]], out=t2[tc][0:Pn, :], in0=xst[0:Pn, :], in1=tsin[0:Pn, sl], op=ALU.mult)
        P.g("dve", "tensor_tensor", [t1[tc], t2[tc]], [s], out=s[0:Pn, sl], in0=t1[tc][0:Pn, :], in1=t2[tc][0:Pn, :], op=ALU.add)

    def epi_rope64(dst_ap, dst_tile, scale):
        s = next_stg()

        def epi(tc, ps):
            scopy(xf[tc][0:64, :], ps[0:64, :], scale, [ps], [xf[tc]])
            rope_tail(tc, xf[tc], 64, s)
            if tc == 1:
                P.dma("sp", dst_ap, s[0:64, :], [s], [dst_tile])
        return epi

    def epi_headnorm_rope(dst_ap, dst_tile, scale, gt):
        s = next_stg()

        def epi(tc, ps):
            sqt = sq[tc]
            scopy(t1[tc][:, :], ps[:, :], scale, [ps], [t1[tc]])
            P.g("act", "activation", [t1[tc]], [sqt], out=sqt[:, :], in_=t1[tc][:, :], func=AF.Square, scale=float(1.0 / scale))
            P.mm(sp_[tc][:, :], ones[:, :], sqt[:, :], True, True, [ones, sqt], [sp_[tc]])
            sl = slice(tc * 512, (tc + 1) * 512)
            P.g("act", "activation", [sp_[tc], epst], [rs], out=rs[:, sl], in_=sp_[tc][:, :], func=AF.Sqrt, bias=epst[:], scale=1.0 / 128)
            P.g("dve", "reciprocal", [rs], [rs], out=rs[:, sl], in_=rs[:, sl])
            P.g("dve", "tensor_scalar", [t1[tc], gt], [t1[tc]], out=t1[tc][:, :], in0=t1[tc][:, :], scalar1=gt[:, 0:1], scalar2=None, op0=ALU.mult)
            P.g("dve", "tensor_tensor", [t1[tc], rs], [xf[tc]], out=xf[tc][:, :], in0=t1[tc][:, :], in1=rs[:, sl], op=ALU.mult)
            rope_tail(tc, xf[tc], 128, s)
            if tc == 1:
                P.dma("sp", dst_ap, s[:, :], [s], [dst_tile])
        return epi

    sA = 192 ** -0.5
    sB = 128 ** -0.5
    P.dma("sp", tcos[0:64, :], D["cosA"][:, :], [], [tcos])
    P.dma("sp", tsin[0:64, :], D["sinA"][:, :], [], [tsin])
    P.dma("pool", wup[:, 0:4 * 2048], D["wukv"][l], [], [wup])
    for c in range(4):
        proj(T_CKV[0] + c, 128, epi_latent(c, 4, 512))
    latent_norm(4, 512, gkva_t)
    for h in range(8):
        upproj(latn, 4, 2048, h * 256, 128, epi_plain(KX[KX_A + h * 128:KX_A + (h + 1) * 128, :], KX, 1.0))
        upproj(latn, 4, 2048, h * 256 + 128, 128, epi_v(VX_A + h * 128))
    proj(T_KPE, 64, epi_rope64(KX[KX_PE:KX_PE + 64, :], KX, 1.0))
    for i in range(12):
        proj(T_BK[0] + i, 128, epi_plain(KX[KX_B + i * 128:KX_B + (i + 1) * 128, :], KX, 1.0))
    for i in range(12):
        proj(T_BV[0] + i, 128, epi_v(VX_B + i * 128))
    for i in range(2):
        proj(T_DK[0] + i, 128, epi_plain(KX[KX_D + i * 128:KX_D + (i + 1) * 128, :], KX, 1.0))
    for i in range(2):
        proj(T_DV[0] + i, 128, epi_v(VX_D + i * 128))
    for i in range(2):
        proj(T_CV[0] + i, 128, epi_v(VX_C + i * 128))
    P.dma("pool", wup[:, :], D["wuq"][l], [], [wup])
    for c in range(6):
        proj(T_CQ[0] + c, 128, epi_latent(c, 6, 768))
    latent_norm(6, 768, gqa_t)
    for h in range(8):
        upproj(latn, 6, 1536, h * 192, 128, epi_plain(QA[h, 0:128, :], QA, sA))
        upproj(latn, 6, 1536, h * 192 + 128, 64, epi_rope64(QA[h, 128:192, :], QA, sA))
    P.dma("sp", tcos[:, :], D["cosC"][:, :], [], [tcos])
    P.dma("sp", tsin[:, :], D["sinC"][:, :], [], [tsin])
    for i in range(2):
        proj(T_CK[0] + i, 128, epi_headnorm_rope(KX[KX_C + i * 128:KX_C + (i + 1) * 128, :], KX, 1.0, gkn_t))
    for c in range(13):
        rows = 64 if c == 12 else 256
        P.coll("agk", KX[c * 256:c * 256 + rows, :], D["KXall"][c, 0:4 * rows, :], GROUPS, [KX], [D["KXall"]])
    for c in range(12):
        P.coll("agv", VX[c], D["VXall"][c], GROUPS, [VX], [D["VXall"]])
    for i in range(8):
        proj(T_CQh[0] + i, 128, epi_headnorm_rope(QC[i, :, :], QC, sB, gqn_t))
    for i in range(12):
        proj(T_BQ[0] + i, 128, epi_plain(QB[i, :, :], QB, sB))
    for i in range(8):
        proj(T_DQ[0] + i, 128, epi_plain(QD[i, :, :], QD, sB))
    for i in range(36):
        proj(T_G[0] + i, 128, epi_silu(G[i, :, :], G))


def phase_T1(P, D, C, l):
    P.phase_reset()
    QA, QB, QC, QD, G, KX, VX = D["QA"], D["QB"], D["QC"], D["QD"], D["G"], D["KX"], D["VX"]
    KXall, VXall, MT, VBext = D["KXall"], D["VXall"], D["MT"], D["VBext"]
    ones, onesL, onesR, selM = C["ones"], C["onesL"], C["onesR"], C["selM"]
    MD, MB = C["MD"], C["MB"]

    q0 = [P.asb(f"q0{i}", [128, S_LOC], BF16) for i in range(2)]
    q1 = [P.asb(f"q1{i}", [128, S_LOC], BF16) for i in range(2)]
    kn = [P.asb(f"kn{i}", [128, 4096], BF16) for i in range(2)]
    kpe = P.asb("kpe", [128, 4096], BF16)
    vv = [P.asb(f"vv{i}", [128, 32, 128], BF16) for i in range(2)]
    pts = [P.asb(f"pt{i}", [128, 512], BF16) for i in range(3)]
    gts = [P.asb(f"gt{i}", [128, S_LOC], BF16) for i in range(2)]
    rinv = [P.asb(f"rinv{i}", [128, 512], F32) for i in range(2)]
    otmp = [P.asb(f"otmp{i}", [128, 512], F32) for i in range(2)]
    ost = [P.asb(f"ost{i}", [128, S_LOC], BF16) for i in range(2)]
    esk = P.asb("esk", [128, 8], F32)
    qd4 = P.asb("qd4", [128, 4, S_LOC], BF16)
    gt4 = P.asb("gt4", [128, 4, S_LOC], BF16)
    ost4 = P.asb("ost4", [128, 4, S_LOC], BF16)
    kdh = P.asb("kdh", [128, 1280], BF16)
    vdh = P.asb("vdh", [128, 10, 128], BF16)
    tmpd = [P.asb(f"tmpd{i}", [128, 512], F32) for i in range(3)]
    ptd = [P.asb(f"ptd{i}", [128, 512], BF16) for i in range(3)]
    kb_t = P.asb("kb_t", [128, 3072], BF16)
    vb_t = P.asb("vb_t", [128, 48, 128], BF16)
    qb_t = P.asb("qb_t", [128, S_LOC], BF16)
    num = [P.asb(f"num{i}", [128, S_LOC], F32) for i in range(3)]
    Ls = [P.asb(f"Ls{i}", [128, S_LOC], F32) for i in range(3)]
    ltot = P.asb("ltot", [128, S_LOC], F32)
    tmpb = P.asb("tmpb", [128, 192], F32)
    ptb = P.asb("ptb", [128, 192], BF16)
    cand = [P.asb(f"cand{i}", [128, 512], BF16) for i in range(4)]
    selo = [P.asb(f"selo{i}", [128, 512], BF16) for i in range(2)]

    sts = [P.aps(f"st{i}", [128, 512], F32) for i in range(3)]
    Ops = [P.aps(f"Op{i}", [128, 512], F32) for i in range(2)]
    Lps = [P.aps(f"Lp{i}", [128, 512], F32) for i in range(2)]
    selp = P.aps("selp", [128, 512], F32)

    P.dma("sp", esk[:, :], D["sinksb"][l], [], [esk])
    P.g("act", "activation", [esk], [esk], out=esk[:, :], in_=esk[:, :], func=AF.Exp)

    st = {"h": 0, "ol": 0, "fin": 0, "sel": 0}

    def select(side, srcs, npart, ncols, dst_ap, dst_tile):
        for r in range(4):
            P.dma("sp", cand[r][0:npart, 0:ncols], srcs[r], [KXall, VXall], [cand[r]])
        for r in range(4):
            P.mm(selp[0:npart, 0:ncols], selM[0:npart, (side * 4 + r) * 128:(side * 4 + r) * 128 + npart], cand[r][0:npart, 0:ncols],
                 r == 0, r == 3, [selM, cand[r]], [selp])
        P.g("dve", "tensor_copy", [selp], [dst_tile], out=dst_ap, in_=selp[0:npart, 0:ncols])

    def finalize(Ot, Lt, esk_cols=None):
        k = st["fin"] % 2
        st["fin"] += 1
        ri = rinv[k]
        if esk_cols is None:
            P.g("dve", "reciprocal", [Lt], [ri], out=ri[:, :], in_=Lt[:, :])
        else:
            for i, hc in enumerate(esk_cols):
                P.g("dve", "tensor_scalar", [Lt, esk], [ri], out=ri[:, i * 128:(i + 1) * 128], in0=Lt[:, i * 128:(i + 1) * 128],
                    scalar1=esk[:, hc:hc + 1], scalar2=None, op0=ALU.add)
            P.g("dve", "reciprocal", [ri], [ri], out=ri[:, :], in_=ri[:, :])
        P.g("dve", "tensor_tensor", [Ot, ri], [otmp[k]], out=otmp[k][:, :], in0=Ot[:, :], in1=ri[:, :], op=ALU.mult)
        return otmp[k]

    def dense_head(qparts, kparts, vt, gt, mt_idx):
        os_ = ost[st["h"] % 2]
        st["h"] += 1
        for qc in range(2):
            Ot = Ops[st["ol"] % 2]
            Lt = Lps[st["ol"] % 2]
            st["ol"] += 1

            def qk(kb):
                s_ = sts[kb % 2]
                n = len(qparts)
                for i in range(n):
                    qt, kp = qparts[i]
                    kt = kparts[i]
                    P.mm(s_[:, :], kt[0:kp, kb * 128:(kb + 1) * 128], qt[0:kp, qc * 512:(qc + 1) * 512], i == 0, i == n - 1,
                         [kt, qt], [s_])
            qk(0)
            for kb in range(32):
                if kb + 1 < 32:
                    qk(kb + 1)
                s_ = sts[kb % 2]
                pt = pts[kb % 3]
                P.g("act", "activation", [s_], [pt], out=pt[:, :], in_=s_[:, :], func=AF.Exp)
                P.mm(Ot[:, :], vt[:, kb, :], pt[:, :], kb == 0, kb == 31, [vt, pt], [Ot])
                P.mm(Lt[:, :], ones[:, :], pt[:, :], kb == 0, kb == 31, [ones, pt], [Lt])
            o = finalize(Ot, Lt)
            P.g("dve", "tensor_tensor", [o, gt], [os_], out=os_[:, qc * 512:(qc + 1) * 512], in0=o[:, :],
                in1=gt[:, qc * 512:(qc + 1) * 512], op=ALU.mult)
        P.dma("sp", MT[mt_idx, :, :], os_[:, :], [os_], [MT])

    def load_k_all(dst, npart, row0):
        for r in range(4):
            P.dma("sp", dst[0:npart, r * 1024:(r + 1) * 1024], kxall(D, r, row0, npart), [KXall], [dst])

    def load_v_all(dst, col0):
        for r in range(4):
            P.dma("sp", dst[:, r * 8:(r + 1) * 8, :],
                  vxall(D, r, 0, 1024, col0, 128).rearrange("(b p) d -> p b d", p=128), [VXall], [dst])

    load_k_all(kpe, 64, KX_PE)
    for h in range(8):
        b = h % 2
        P.dma("sp", q0[b][:, :], QA[h, 0:128, :], [QA], [q0[b]])
        P.dma("sp", q1[b][0:64, :], QA[h, 128:192, :], [QA], [q1[b]])
        load_k_all(kn[b], 128, KX_A + h * 128)
        load_v_all(vv[b], VX_A + h * 128)
        P.dma("sp", gts[b][:, :], G[h, :, :], [G], [gts[b]])
        dense_head([(q0[b], 128), (q1[b], 64)], [kn[b], kpe], vv[b], gts[b], h)
    for g in range(2):
        load_k_all(kn[g], 128, KX_C + g * 128)
        load_v_all(vv[g], VX_C + g * 128)
        for i in range(4):
            hq = g * 4 + i
            b = hq % 2
            P.dma("sp", q0[b][:, :], QC[hq, :, :], [QC], [q0[b]])
            P.dma("sp", gts[b][:, :], G[20 + hq, :, :], [G], [gts[b]])
            dense_head([(q0[b], 128)], [kn[g]], vv[g], gts[b], 20 + hq)
    for g in range(2):
        for i in range(4):
            P.dma("sp", qd4[:, i, :], QD[g * 4 + i, :, :], [QD], [qd4])
            P.dma("sp", gt4[:, i, :], G[28 + g * 4 + i, :, :], [G], [gt4])
        krow = KX_D + g * 128
        vcol = VX_D + g * 128
        P.dma("sp", kdh[:, 128:1152], KX[krow:krow + 128, :], [KX], [kdh])
        P.dma("sp", vdh[:, 1:9, :], vx(D, vcol, 128).rearrange("(b p) d -> p b d", p=128), [VX], [vdh])
        select(0, [kxall(D, r, krow, 128)[:, 896:1024] for r in range(4)], 128, 128, kdh[:, 0:128], kdh)
        select(1, [kxall(D, r, krow, 128)[:, 0:128] for r in range(4)], 128, 128, kdh[:, 1152:1280], kdh)
        select(0, [vxall(D, r, 896, 1024, vcol, 128) for r in range(4)], 128, 128, vdh[:, 0, :], vdh)
        select(1, [vxall(D, r, 0, 128, vcol, 128) for r in range(4)], 128, 128, vdh[:, 9, :], vdh)
        for qb in range(8):
            Ot = Ops[st["ol"] % 2]
            Lt = Lps[st["ol"] % 2]
            st["ol"] += 1
            qsl = slice(qb * 128, (qb + 1) * 128)
            for j in range(3):
                P.mm(sts[j][:, :], kdh[:, (qb + j) * 128:(qb + j + 1) * 128], qd4[:, :, qsl], True, True, [kdh, qd4], [sts[j]])
                off = (g * 3 + j) * 512
                P.g("dve", "tensor_tensor", [sts[j], MD], [tmpd[j]], out=tmpd[j][:, :], in0=sts[j][:, :], in1=MD[:, off:off + 512], op=ALU.add)
                P.g("act", "activation", [tmpd[j]], [ptd[j]], out=ptd[j][:, :], in_=tmpd[j][:, :], func=AF.Exp)
            for j in range(3):
                kbi = qb + j
                on = onesL if kbi == 0 else (onesR if kbi == 9 else ones)
                P.mm(Ot[:, :], vdh[:, kbi, :], ptd[j][:, :], j == 0, j == 2, [vdh, ptd[j]], [Ot])
                P.mm(Lt[:, :], on[:, :], ptd[j][:, :], j == 0, j == 2, [on, ptd[j]], [Lt])
            o = finalize(Ot, Lt, esk_cols=[g * 4 + i for i in range(4)])
            P.g("dve", "tensor_tensor", [o, gt4], [ost4], out=ost4[:, :, qsl], in0=o[:, :].rearrange("p (a b) -> p a b", a=4),
                in1=gt4[:, :, qsl], op=ALU.mult)
        for i in range(4):
            P.dma("sp", MT[28 + g * 4 + i, :, :], ost4[:, i, :], [ost4], [MT])
    for p in range(3):
        d = DILS[p]
        hl = 64 * d
        vc0 = VX_B + p * 512
        for q in range(2):
            P.dma("sp", VBext[p][hl:hl + 1024, q * 256:(q + 1) * 256], vx(D, vc0 + q * 256, 256), [VX], [VBext[p]], semtile=VBext[p])
        for side in range(2):
            t0 = 0
            while t0 < hl:
                nt = min(128, hl - t0)
                so = selo[st["sel"] % 2]
                st["sel"] += 1
                for q in range(2):
                    if side == 0:
                        srcs = [vxall(D, r, 1024 - hl + t0, 1024 - hl + t0 + nt, vc0 + q * 256, 256) for r in range(4)]
                    else:
                        srcs = [vxall(D, r, t0, t0 + nt, vc0 + q * 256, 256) for r in range(4)]
                    select(side, srcs, nt, 256, so[0:nt, q * 256:(q + 1) * 256], so)
                drow = t0 if side == 0 else hl + 1024 + t0
                P.dma("sp", VBext[p][drow:drow + nt, :], so[0:nt, 0:512], [so], [VBext[p]])
                t0 += nt
    stb = sts[0]
    for hh in range(4):
        for p in range(3):
            dil = DILS[p]
            hl = 64 * dil
            nq = 1024 // dil
            nqc = nq // 64
            nkc = nqc + 2
            hidx = p * 4 + hh
            krow = KX_B + hidx * 128
            P.dma("sp", qb_t[:, :], QB[hidx, :, :], [QB], [qb_t])
            P.dma("sp", kb_t[:, hl:hl + 1024], KX[krow:krow + 128, :], [KX], [kb_t])
            c0 = 0
            while c0 < hl:
                ncol = min(512, hl - c0)
                select(0, [kxall(D, r, krow, 128)[:, 1024 - hl + c0:1024 - hl + c0 + ncol] for r in range(4)],
                       128, ncol, kb_t[:, c0:c0 + ncol], kb_t)
                select(1, [kxall(D, r, krow, 128)[:, c0:c0 + ncol] for r in range(4)],
                       128, ncol, kb_t[:, hl + 1024 + c0:hl + 1024 + c0 + ncol], kb_t)
                c0 += ncol
            vsrc = VBext[p][:, hh * 128:(hh + 1) * 128].rearrange("(c k r) dv -> r k c dv", k=64, r=dil)
            for r in range(dil):
                P.dma("sp", vb_t[0:64, r * nkc:(r + 1) * nkc, :], vsrc[r], [VBext[p]], [vb_t])
            qi = 0
            bank = []
            for r in range(dil):
                for i in range(nqc):
                    if qi % 8 == 0:
                        Ot = Ops[st["ol"] % 2]
                        Lt = Lps[st["ol"] % 2]
                        st["ol"] += 1
                        bank = []
                    s0 = r + dil * 64 * i
                    qsl = slice(s0, s0 + dil * 63 + 1, dil)
                    for j in range(3):
                        k0 = r + dil * 64 * (i + j)
                        P.mm(stb[0:64, j * 64:(j + 1) * 64], kb_t[:, k0:k0 + dil * 63 + 1:dil], qb_t[:, qsl], True, True,
                             [kb_t, qb_t], [stb])
                    P.g("dve", "tensor_tensor", [stb, MB], [tmpb], out=tmpb[0:64, :], in0=stb[0:64, 0:192],
                        in1=MB[0:64, hidx * 192:(hidx + 1) * 192], op=ALU.add)
                    P.g("act", "activation", [tmpb], [ptb], out=ptb[0:64, :], in_=tmpb[0:64, :], func=AF.Exp)
                    col = (qi % 8) * 64
                    kc0 = r * nkc + i
                    for j in range(3):
                        on = onesL if (i + j == 0) else (onesR if (i + j == nkc - 1) else ones)
                        P.mm(Ot[:, col:col + 64], vb_t[0:64, kc0 + j, :], ptb[0:64, j * 64:(j + 1) * 64], j == 0, j == 2, [vb_t, ptb], [Ot])
                        P.mm(Lt[:, col:col + 64], on[0:64, :], ptb[0:64, j * 64:(j + 1) * 64], j == 0, j == 2, [on, ptb], [Lt])
                    bank.append((col, qsl))
                    qi += 1
                    if qi % 8 == 0:
                        for (cc, qs) in bank:
                            P.g("dve", "tensor_copy", [Ot], [num[p]], out=num[p][:, qs], in_=Ot[:, cc:cc + 64])
                            P.g("dve", "tensor_copy", [Lt], [Ls[p]], out=Ls[p][:, qs], in_=Lt[:, cc:cc + 64])
        P.g("dve", "tensor_tensor", [Ls[0], Ls[1]], [ltot], out=ltot[:, :], in0=Ls[0][:, :], in1=Ls[1][:, :], op=ALU.add)
        P.g("dve", "tensor_tensor", [ltot, Ls[2]], [ltot], out=ltot[:, :], in0=ltot[:, :], in1=Ls[2][:, :], op=ALU.add)
        P.g("dve", "reciprocal", [ltot], [ltot], out=ltot[:, :], in_=ltot[:, :])
        for p in range(3):
            b = p % 2
            P.dma("sp", gts[b][:, :], G[8 + p * 4 + hh, :, :], [G], [gts[b]])
            P.g("dve", "tensor_tensor", [num[p], ltot], [num[p]], out=num[p][:, :], in0=num[p][:, :], in1=ltot[:, :], op=ALU.mult)
            os_ = ost[b]
            P.g("dve", "tensor_tensor", [num[p], gts[b]], [os_], out=os_[:, :], in0=num[p][:, :], in1=gts[b][:, :], op=ALU.mult)
            P.dma("sp", MT[8 + p * 4 + hh, :, :], os_[:, :], [os_], [MT])


def phase_T2(P, D, C, l, final):
    P.phase_reset()
    MT = D["MT"]
    x = D["x"] if l == 0 else D["x1"]
    wo = D["wo"]
    epst = C["epst"]
    dst = D["xsc"] if final else D["x1"]
    xo = D["xo"]
    mt = P.asb("mt", [128, 36, S_LOC], BF16)
    wob = [P.asb(f"wob{i}", [128, 36, 256], BF16) for i in range(2)]
    xt = [P.asb(f"xt{i}", [128, 256], F32) for i in range(3)]
    yt = [P.asb(f"yt{i}", [128, 256], F32) for i in range(3)]
    ps = [P.aps(f"ps{i}", [128, 512], F32) for i in range(4)]
    mts = [P.sub(mt, f"mt{c}") for c in range(36)]
    for c in range(36):
        P.dma("sp", mt[:, c, :], MT[c, :, :], [MT], [mts[c]], semtile=mt)
    n = 0
    xr = [] if l == 0 else [x]
    for dc in range(16):
        wb = wob[dc % 2]
        P.dma("pool", wb[:, :, :], wo[l, dc].rearrange("p (c n) -> p c n", c=36), [], [wb])
        for tt in range(8):
            pt = ps[n % 4]
            xt_ = xt[n % 3]
            yt_ = yt[n % 3]
            n += 1
            P.dma("sp", xt_[:, :], x[tt * 128:(tt + 1) * 128, dc * 256:(dc + 1) * 256], xr, [xt_])
            for c in range(36):
                P.mm(pt[:, 0:256], mt[:, c, tt * 128:(tt + 1) * 128], wb[:, c, :], c == 0, c == 35, [mts[c], wb], [pt])
            P.g("dve", "tensor_tensor", [pt, xt_], [yt_], out=yt_[:, :], in0=pt[:, 0:256], in1=xt_[:, :], op=ALU.add)
            P.dma("sp", dst[tt * 128:(tt + 1) * 128, dc * 256:(dc + 1) * 256], yt_[:, :], [yt_], [dst])
    if final:
        xf = P.asb("xff", [128, 4096], F32)
        gf = P.asb("gf", [128, 4096], F32)
        of = P.asb("of", [128, 4096], F32)
        junk = P.asb("junk2", [128, 512], F32)
        ssq8 = P.asb("ssq8b", [128, 8], F32)
        ssq = P.asb("ssqb", [128, 2], F32)
        P.dma("sp", gf[:, :], D["gfin"][:, :], [], [gf])
        for tt in range(8):
            P.dma("sp", xf[:, :], dst[tt * 128:(tt + 1) * 128, :], [dst], [xf])
            P.g("dve", "memset", [], [ssq8], ssq8[:, :], 0.0)
            for k in range(8):
                P.g("act", "activation", [xf, ssq8], [junk, ssq8], out=junk[:, :], in_=xf[:, k * 512:(k + 1) * 512], func=AF.Square,
                    accum_out=ssq8[:, k:k + 1])
            P.g("dve", "reduce_sum", [ssq8], [ssq], out=ssq[:, 0:1], in_=ssq8[:, 0:8], axis=AX.X)
            P.g("act", "activation", [ssq, epst], [ssq], out=ssq[:, 0:1], in_=ssq[:, 0:1], func=AF.Sqrt, bias=epst[:], scale=1.0 / 4096)
            P.g("dve", "reciprocal", [ssq], [ssq], out=ssq[:, 0:1], in_=ssq[:, 0:1])
            P.g("dve", "tensor_scalar", [xf, ssq], [xf], out=xf[:, :], in0=xf[:, :], scalar1=ssq[:, 0:1], scalar2=None, op0=ALU.mult)
            P.g("dve", "tensor_tensor", [xf, gf], [of], out=of[:, :], in0=xf[:, :], in1=gf[:, :], op=ALU.mult)
            P.dma("sp", xo[tt * 128:(tt + 1) * 128, :], of[:, :], [of], [])


def build_fused(nlayers=2, phases="PAT", final_last=True):
    nc = bass.Bass("TRN2", target_bir_lowering=False)
    P = Prog(nc)
    IN = "ExternalInput"
    D = {}
    D["x"] = P.dram("x", [S_LOC, 4096], F32, IN)
    D["wt"] = P.dram("wt", [nlayers, NW, 128, 4096], F32, IN)
    D["wo"] = P.dram("wo", [nlayers, 16, 128, 36 * 256], F32, IN)
    D["gattn"] = P.dram("gattn", [nlayers, 128, 32], F32, IN)
    D["gqa"] = P.dram("gqa", [nlayers, 128, 6], F32, IN)
    D["gkva"] = P.dram("gkva", [nlayers, 128, 4], F32, IN)
    D["gqn"] = P.dram("gqn", [nlayers, 128, 1], F32, IN)
    D["gkn"] = P.dram("gkn", [nlayers, 128, 1], F32, IN)
    D["wuq"] = P.dram("wuq", [nlayers, 128, 6 * 1536], F32, IN)
    D["wukv"] = P.dram("wukv", [nlayers, 128, 4 * 2048], F32, IN)
    D["cosA"] = P.dram("cosA", [64, S_LOC], F32, IN)
    D["sinA"] = P.dram("sinA", [64, S_LOC], F32, IN)
    D["cosC"] = P.dram("cosC", [128, S_LOC], F32, IN)
    D["sinC"] = P.dram("sinC", [128, S_LOC], F32, IN)
    D["sinksb"] = P.dram("sinksb", [nlayers, 128, 8], F32, IN)
    D["gfin"] = P.dram("gfin", [128, 4096], F32, IN)
    MDb = P.dram("MDb", [128, 2 * 3 * 512], F32, IN)
    MDm = P.dram("MDm", [128, 2 * 3 * 512], F32, IN)
    MBb = P.dram("MBb", [64, 12 * 192], F32, IN)
    MBm = P.dram("MBm", [64, 12 * 192], F32, IN)
    onesL_d = P.dram("onesL", [128, 128], F32, IN)
    onesR_d = P.dram("onesR", [128, 128], F32, IN)
    sel_d = P.dram("selM", [128, 8 * 128], F32, IN)
    D["xo"] = P.dram("xo", [S_LOC, 4096], F32, "ExternalOutput")
    D["QA"] = P.dram("QA", [8, 192, S_LOC], BF16)
    D["QB"] = P.dram("QB", [12, 128, S_LOC], BF16)
    D["QC"] = P.dram("QC", [8, 128, S_LOC], BF16)
    D["QD"] = P.dram("QD", [8, 128, S_LOC], BF16)
    D["G"] = P.dram("G", [36, 128, S_LOC], BF16)
    D["KX"] = P.dram("KX", [KX_ROWS, S_LOC], BF16)
    D["VX"] = P.dram("VX", [12, S_LOC, 256], BF16)
    D["KXall"] = P.dram("KXall", [13, 1024, S_LOC], BF16)
    D["VXall"] = P.dram("VXall", [12, 4 * S_LOC, 256], BF16)
    D["VBext"] = [P.dram(f"VBext{p}", [NKB[p], 512], BF16) for p in range(3)]
    D["MT"] = P.dram("MT", [36, 128, S_LOC], BF16)
    D["x1"] = P.dram("x1", [S_LOC, 4096], F32)
    D["xsc"] = P.dram("xsc", [S_LOC, 4096], F32)
    if nlayers == 1 and not final_last:
        D["xo"].multi = True
        D["x1"] = D["xo"]

    C = {}
    C["ident"] = P.sbuf("ident", [128, 128], BF16)
    identf = P.sbuf("identf", [128, 128], F32)
    C["ones"] = P.sbuf("ones", [128, 128], BF16)
    C["onesL"] = P.sbuf("onesLs", [128, 128], BF16)
    C["onesR"] = P.sbuf("onesRs", [128, 128], BF16)
    C["selM"] = P.sbuf("selMs", [128, 8 * 128], BF16)
    C["epst"] = P.sbuf("epst", [128, 1], F32)
    C["MD"] = P.sbuf("MD", [128, 2 * 3 * 512], F32)
    C["MB"] = P.sbuf("MB", [128, 12 * 192], F32)
    P.init_arena(ARENA_N32)
    P.g("pool", "memset", [], [identf], identf[:], 1.0)
    P.g("pool", "affine_select", [identf], [identf], out=identf[:], in_=identf[:], pattern=[[-1, 128]],
        compare_op=ALU.is_equal, fill=0.0, base=0, channel_multiplier=1)
    P.g("dve", "tensor_copy", [identf], [C["ident"]], out=C["ident"][:], in_=identf[:])
    P.g("pool", "memset", [], [C["ones"]], C["ones"][:], 1.0)
    P.g("pool", "memset", [], [C["epst"]], C["epst"][:], EPS)
    P.dma("pool", C["onesL"][:], onesL_d[:], [], [C["onesL"]])
    P.dma("pool", C["onesR"][:], onesR_d[:], [], [C["onesR"]])
    P.dma("pool", C["selM"][:], sel_d[:], [], [C["selM"]])
    P.phase_reset()
    md2 = P.asb("md2", [128, 2 * 3 * 512], F32)
    mb2 = P.asb("mb2", [128, 12 * 192], F32)
    P.dma("sp", C["MD"][:], MDb[:], [], [C["MD"]])
    P.dma("sp", md2[:, :], MDm[:], [], [md2])
    P.g("dve", "tensor_tensor", [C["MD"], md2], [C["MD"]], out=C["MD"][:], in0=C["MD"][:], in1=md2[:, :], op=ALU.add)
    P.dma("sp", C["MB"][0:64, :], MBb[:], [], [C["MB"]])
    P.dma("sp", mb2[0:64, :], MBm[:], [], [mb2])
    P.g("dve", "tensor_tensor", [C["MB"], mb2], [C["MB"]], out=C["MB"][0:64, :], in0=C["MB"][0:64, :], in1=mb2[0:64, :], op=ALU.add)

    for l in range(nlayers):
        if "P" in phases:
            phase_P(P, D, C, l)
        if "A" in phases:
            phase_T1(P, D, C, l)
        if "T" in phases:
            phase_T2(P, D, C, l, final_last and (l == nlayers - 1))
    P.emit()
    P.close()
    return nc


BF = ml_dtypes.bfloat16
S = 4096
PERM_C = np.concatenate([np.arange(0, 32), np.arange(64, 96), np.arange(32, 64), np.arange(96, 128)])


def rope_angles(pos, dim):
    inv = (10000.0 ** (-np.arange(0, dim, 2, dtype=np.float32) / dim)).astype(np.float32)
    return pos.astype(np.float32)[:, None] * inv[None, :]


def rope_tables(j):
    pos = np.arange(j * S_LOC, (j + 1) * S_LOC)
    at = rope_angles(pos, 64)
    ar = rope_angles(pos // 64, 64)
    ac = rope_angles(pos % 64, 64)
    cosA = np.concatenate([np.cos(at), np.cos(at)], 1).T
    sinA = np.concatenate([-np.sin(at), np.sin(at)], 1).T
    cosC = np.concatenate([np.cos(ar), np.cos(ac), np.cos(ar), np.cos(ac)], 1).T
    sinC = np.concatenate([-np.sin(ar), -np.sin(ac), np.sin(ar), np.sin(ac)], 1).T
    f = lambda a: np.ascontiguousarray(a, dtype=np.float32)
    return f(cosA), f(sinA), f(cosC), f(sinC)


def layer_weights_P(inp, l):
    w_in = inp["w_in"][l]
    cols = []
    for i in range(10):
        cols.append(np.arange(i * 128, (i + 1) * 128))
    tiles = np.zeros((107, 128, 32, 128), np.float32)
    w3 = w_in.reshape(32, 128, 13632)
    for i in range(10):
        tiles[i] = w3[:, :, i * 128:(i + 1) * 128].transpose(1, 0, 2)
    tiles[10, :, :, 0:64] = w3[:, :, 1280:1344].transpose(1, 0, 2)
    for t in range(11, 107):
        c0 = 1344 + (t - 11) * 128
        blk = w3[:, :, c0:c0 + 128].transpose(1, 0, 2)
        if 47 <= t < 57:
            blk = blk[:, :, PERM_C]
        tiles[t] = blk
    d = {}
    d["wt"] = tiles.reshape(107, 128, 4096)
    d["gattn"] = np.ascontiguousarray(inp["g_attn"][l].reshape(32, 128).T)
    d["gqa"] = np.ascontiguousarray(inp["g_qa"][l].reshape(6, 128).T)
    d["gkva"] = np.ascontiguousarray(inp["g_kva"][l].reshape(4, 128).T)
    d["gqn"] = np.ascontiguousarray(inp["g_qn"][l][PERM_C].reshape(128, 1))
    d["gkn"] = np.ascontiguousarray(inp["g_kn"][l][PERM_C].reshape(128, 1))
    d["wuq"] = np.ascontiguousarray(inp["w_uq"][l].reshape(6, 128, 1536).transpose(1, 0, 2).reshape(128, 6 * 1536))
    d["wukv"] = np.ascontiguousarray(inp["w_ukv"][l].reshape(4, 128, 2048).transpose(1, 0, 2).reshape(128, 4 * 2048))
    return d


def in_maps_P(inp, l, xcur):
    wd = layer_weights_P(inp, l)
    maps = []
    for c in range(8):
        b, j = divmod(c, 4)
        m = dict(wd)
        m["x"] = np.ascontiguousarray(xcur[b, j * S_LOC:(j + 1) * S_LOC, :])
        m["cosA"], m["sinA"], m["cosC"], m["sinC"] = rope_tables(j)
        maps.append(m)
    return maps


def rel_bucket_np(rel):
    nb = 16
    me = 8
    ret = np.where(rel > 0, nb, 0)
    n = np.abs(rel)
    nf = np.maximum(n, 1).astype(np.float32)
    large = me + (np.log(nf / np.float32(me)) / np.float32(math.log(1024 / me)) * np.float32(nb - me)).astype(np.int32)
    large = np.minimum(large, nb - 1)
    return ret + np.where(n < me, n, large)


def bias_tables(rel_bias):
    k = np.arange(128)[:, None]
    q = np.arange(128)[None, :]
    MDb = np.zeros((128, 2, 3, 4, 128), np.float32)
    MDm = np.zeros((128, 2, 3, 4, 128), np.float32)
    for jj in range(3):
        rel = (jj - 1) * 128 + k - q
        bk = rel_bucket_np(rel)
        ok = np.abs(rel) <= 128
        for g in range(2):
            for i in range(4):
                MDb[:, g, jj, i, :] = np.where(ok, rel_bias[bk, 12 + g * 4 + i], 0.0)
                MDm[:, g, jj, i, :] = np.where(ok, 0.0, -30000.0)
    k = np.arange(64)[:, None]
    q = np.arange(64)[None, :]
    MBb = np.zeros((64, 12, 3, 64), np.float32)
    MBm = np.zeros((64, 12, 3, 64), np.float32)
    for p, dil in enumerate((1, 4, 16)):
        for jj in range(3):
            rel = (jj - 1) * 64 + k - q
            bk = rel_bucket_np(dil * rel)
            ok = np.abs(rel) <= 64
            for hh in range(4):
                MBb[:, p * 4 + hh, jj, :] = np.where(ok, rel_bias[bk, p * 4 + hh], 0.0)
                MBm[:, p * 4 + hh, jj, :] = np.where(ok, 0.0, -30000.0)
    return (MDb.reshape(128, -1), MDm.reshape(128, -1), MBb.reshape(64, -1), MBm.reshape(64, -1))


def fused_in_maps(inp, layers=(0, 1), xcur=None):
    MDb, MDm, MBb, MBm = bias_tables(inp["rel_bias"])
    ws = [layer_weights_P(inp, l) for l in layers]
    nl = len(layers)
    if xcur is None:
        xcur = inp["x"]
    shared = {}
    for k in ("wt", "gattn", "gqa", "gkva", "gqn", "gkn", "wuq", "wukv"):
        shared[k] = np.ascontiguousarray(np.stack([w[k] for w in ws], 0))
    wo = inp["w_out"][list(layers)].reshape(nl, 36, 128, 16, 256).transpose(0, 3, 2, 1, 4).reshape(nl, 16, 128, 36 * 256)
    shared["wo"] = np.ascontiguousarray(wo)
    shared["sinksb"] = np.ascontiguousarray(np.broadcast_to(inp["sinks"][list(layers)][:, None, :], (nl, 128, 8)), dtype=np.float32)
    shared["gfin"] = np.ascontiguousarray(np.broadcast_to(inp["g_final"][None, :], (128, 4096)), dtype=np.float32)
    shared["MDb"], shared["MDm"], shared["MBb"], shared["MBm"] = MDb, MDm, MBb, MBm
    eye = np.eye(128, dtype=np.float32)
    maps = []
    for c in range(8):
        b, j = divmod(c, 4)
        m = dict(shared)
        m["x"] = np.ascontiguousarray(xcur[b, j * S_LOC:(j + 1) * S_LOC, :])
        m["cosA"], m["sinA"], m["cosC"], m["sinC"] = rope_tables(j)
        m["onesL"] = np.full((128, 128), 1.0 if j > 0 else 0.0, np.float32)
        m["onesR"] = np.full((128, 128), 1.0 if j < 3 else 0.0, np.float32)
        sel = np.zeros((128, 8, 128), np.float32)
        for r in range(4):
            if r == j - 1:
                sel[:, r, :] = eye
            if r == j + 1:
                sel[:, 4 + r, :] = eye
        m["selM"] = sel.reshape(128, 8 * 128)
        maps.append(m)
    return maps


_NC = {}


def kernel(x, g_attn, w_in, g_qa, g_kva, w_uq, w_ukv, g_qn, g_kn, sinks, w_out, rel_bias, g_final):
    f = lambda a: np.asarray(a, np.float32)
    inp = dict(x=f(x), g_attn=f(g_attn), w_in=f(w_in), g_qa=f(g_qa), g_kva=f(g_kva), w_uq=f(w_uq), w_ukv=f(w_ukv),
               g_qn=f(g_qn), g_kn=f(g_kn), sinks=f(sinks), w_out=f(w_out), rel_bias=f(rel_bias), g_final=f(g_final))
    xcur = inp["x"]
    for l in range(2):
        key = "last" if l == 1 else "mid"
        if key not in _NC:
            _NC[key] = build_fused(1, "PAT", l == 1)
        res = run_bass_kernel_spmd(_NC[key], fused_in_maps(inp, (l,), xcur), core_ids=list(range(8))).results
        xcur = np.stack([np.concatenate([np.asarray(res[4 * b + j]["xo"], np.float32) for j in range(4)], 0) for b in range(2)], 0)
    return xcur
```

```python
import contextlib
import math
import numpy as np
import ml_dtypes
from concourse.bass_utils import run_bass_kernel_spmd
import concourse.bass as bass
import concourse.mybir as mybir

F32 = mybir.dt.float32
BF16 = mybir.dt.bfloat16
ALU = mybir.AluOpType
AF = mybir.ActivationFunctionType
AX = mybir.AxisListType

ENGS = ["pe", "act", "dve", "pool", "sp"]


class Tile:
    def __init__(self, name, t):
        self.name = name
        self.t = t
        self.writers = []
        self.readers = []
        self.dsem = None
        self.multi = False

    def __getitem__(self, idx):
        return self.t[idx]


class Op:
    __slots__ = ("eng", "fn", "waits", "signals", "sigval", "dma", "idx")


class Prog:
    def __init__(self, nc, same_engine_sync=True):
        self.nc = nc
        self.stack = contextlib.ExitStack()
        self.ops = {e: [] for e in ENGS}
        self.same_engine_sync = same_engine_sync
        self.dsems = []
        self.dsem_by_name = {}
        self.esem = {}
        self.ntile = 0
        self.arena = None
        self.arena_n32 = 0
        self.bump = 0
        self.banks = None
        self.nbank = 0
        self.live = []
        self.carry = []

    def init_arena(self, n32):
        self.arena = self.stack.enter_context(self.nc.sbuf_tensor("arena", [128, n32], F32))
        self.arena_n32 = n32
        self.banks = [self.stack.enter_context(self.nc.psum_tensor(f"bank{i}", [128, 512], F32)) for i in range(8)]

    def phase_reset(self):
        best = {}
        for tok in self.carry:
            key = (tok[0], tok[1])
            if key not in best or tok[2] > best[key][2]:
                best[key] = tok
        for t in self.live:
            for tok in t.writers + t.readers:
                key = (tok[0], tok[1])
                if key not in best or tok[2] > best[key][2]:
                    best[key] = tok
        self.carry = list(best.values())
        self.live = []
        self.bump = 0
        self.nbank = 0

    def _view(self, base_ap, shape, dt):
        ap = base_ap
        if dt != F32:
            ap = ap.bitcast(dt)
        if len(shape) == 3:
            ap = ap.rearrange("p (a b) -> p a b", a=shape[1])
        return ap

    def asb(self, name, shape, dt):
        esz = 4 if dt == F32 else 2
        nfree = 1
        for d in shape[1:]:
            nfree *= d
        n32 = (nfree * esz + 3) // 4
        n32 = (n32 + 7) // 8 * 8
        assert self.bump + n32 <= self.arena_n32, (name, self.bump, n32)
        ap = self._view(self.arena[:, self.bump:self.bump + (nfree * esz) // 4], shape, dt)
        self.bump += n32
        t = Tile(name, ap)
        t.writers = list(self.carry)
        self.live.append(t)
        return t

    def aps(self, name, shape, dt):
        b = self.banks[self.nbank]
        self.nbank += 1
        assert self.nbank <= 8
        t = Tile(name, self._view(b[:, :], shape, dt))
        t.writers = list(self.carry)
        self.live.append(t)
        return t

    def sem(self, name):
        return self.stack.enter_context(self.nc.semaphore(name))

    def sbuf(self, name, shape, dt):
        t = self.stack.enter_context(self.nc.sbuf_tensor(name, list(shape), dt))
        return Tile(name, t)

    def psum(self, name, shape, dt):
        t = self.stack.enter_context(self.nc.psum_tensor(name, list(shape), dt))
        return Tile(name, t)

    def dram(self, name, shape, dt, kind=None):
        if kind is None:
            t = self.nc.dram_tensor(name, list(shape), dt)
        else:
            t = self.nc.dram_tensor(name, list(shape), dt, kind=kind)
        tl = Tile(name, t)
        tl.multi = kind is None
        return tl

    def sub(self, tile, name=None):
        t = Tile(name or tile.name, tile.t)
        t.writers = list(tile.writers)
        t.readers = list(tile.readers)
        if tile in self.live:
            self.live.append(t)
        return t

    def _deps(self, eng, reads, writes):
        toks = []
        for t in reads:
            toks += t.writers
        for t in writes:
            if not t.multi:
                toks += t.writers
            toks += t.readers
        waits = []
        for tok in toks:
            if tok[0] == "E":
                if tok[1] == eng and (eng == "pe" or not self.same_engine_sync):
                    continue
                waits.append(tok)
            else:
                waits.append(("D", tok[1], self.dsems[tok[1]][1]))
        return waits

    def _commit(self, tok, reads, writes):
        for t in reads:
            t.readers.append(tok)
        for t in writes:
            if t.multi and not t.readers:
                t.writers = [w for w in t.writers if (w[0], w[1]) != (tok[0], tok[1])] + [tok]
            else:
                t.writers = [tok]
                t.readers = []

    def op(self, eng, fn, reads=(), writes=()):
        o = Op()
        o.eng = eng
        o.fn = fn
        o.dma = False
        o.signals = False
        o.sigval = None
        o.waits = self._deps(eng, reads, writes)
        o.idx = len(self.ops[eng])
        self.ops[eng].append(o)
        self._commit(("E", eng, o.idx), reads, writes)
        return o

    def dma(self, eng, out_ap, in_ap, reads=(), writes=(), semtile=None, **kw):
        if semtile is None:
            semtile = (list(writes) + list(reads))[0]
        if semtile.dsem is None:
            if semtile.name in self.dsem_by_name:
                semtile.dsem = self.dsem_by_name[semtile.name]
            else:
                semtile.dsem = len(self.dsems)
                self.dsems.append([self.sem("d_" + semtile.name + str(len(self.dsems))), 0])
                self.dsem_by_name[semtile.name] = semtile.dsem
        sid = semtile.dsem
        o = Op()
        o.eng = eng
        o.dma = True
        o.signals = False
        o.waits = self._deps(eng, reads, writes)
        self.dsems[sid][1] += 16
        val = self.dsems[sid][1]
        semh = self.dsems[sid][0]

        def fn(e, out_ap=out_ap, in_ap=in_ap, semh=semh, kw=kw):
            e.dma_start(out=out_ap, in_=in_ap, **kw).then_inc(semh, 16)

        o.fn = fn
        o.sigval = None
        o.idx = len(self.ops[eng])
        self.ops[eng].append(o)
        self._commit(("D", sid, val), reads, writes)
        return o

    def coll(self, name, in_ap, out_ap, groups, reads, writes):
        if name not in self.dsem_by_name:
            self.dsem_by_name[name] = len(self.dsems)
            self.dsems.append([self.sem("c_" + name), 0])
        sid = self.dsem_by_name[name]
        o = Op()
        o.eng = "pool"
        o.dma = True
        o.signals = False
        o.waits = self._deps("pool", reads, writes)
        self.dsems[sid][1] += 1
        val = self.dsems[sid][1]
        semh = self.dsems[sid][0]

        def fn(e):
            e.collective_compute("AllGather", ALU.bypass, replica_groups=groups, ins=[in_ap], outs=[out_ap]).then_inc(semh)

        o.fn = fn
        o.sigval = None
        o.idx = len(self.ops["pool"])
        self.ops["pool"].append(o)
        self._commit(("D", sid, val), reads, writes)
        return o

    def emit(self, final_waits_engine="sp"):
        nc = self.nc
        for e in ENGS:
            for o in self.ops[e]:
                for w in o.waits:
                    if w[0] == "E":
                        self.ops[w[1]][w[2]].signals = True
        for e in ENGS:
            c = 0
            for o in self.ops[e]:
                if o.signals:
                    c += 1
                    o.sigval = c
            self.esem[e] = self.sem("e_" + e)
        final = [(s[0], s[1]) for s in self.dsems if s[1] > 0]

        def replay(ename, eng):
            seen = {}
            for o in self.ops[ename]:
                for w in o.waits:
                    if w[0] == "E":
                        semh = self.esem[w[1]]
                        key = ("E", w[1])
                        val = self.ops[w[1]][w[2]].sigval
                    else:
                        semh = self.dsems[w[1]][0]
                        key = ("D", w[1])
                        val = w[2]
                    if seen.get(key, 0) >= val:
                        continue
                    seen[key] = val
                    eng.wait_ge(semh, val)
                if o.dma:
                    o.fn(eng)
                else:
                    ins = o.fn(eng)
                    if o.signals:
                        ins.then_inc(self.esem[ename], 1)
            if ename == final_waits_engine:
                for semh, val in final:
                    eng.wait_ge(semh, val)

        with nc.Block() as block:
            @block.tensor
            def _(e):
                replay("pe", e)

            @block.scalar
            def _(e):
                replay("act", e)

            @block.vector
            def _(e):
                replay("dve", e)

            @block.gpsimd
            def _(e):
                replay("pool", e)

            @block.sync
            def _(e):
                replay("sp", e)

    def close(self):
        self.stack.close()


    def g(self, eng, method, reads, writes, *args, **kw):
        return self.op(eng, lambda e: getattr(e, method)(*args, **kw), reads, writes)

    def mm(self, out, lhsT, rhs, start, stop, reads, writes):
        return self.op("pe", lambda e: e.matmul(out, lhsT=lhsT, rhs=rhs, start=start, stop=stop), reads, writes)

    def tr(self, out, in_, ident, reads, writes):
        return self.op("pe", lambda e: e.transpose(out, in_, ident), reads, writes)


S_LOC = 1024
NT = S_LOC // 128
NW = 107
EPS = 1e-6
T_CQ = (0, 6)
T_CKV = (6, 10)
T_KPE = 10
T_BQ = (11, 23)
T_BK = (23, 35)
T_BV = (35, 47)
T_CQh = (47, 55)
T_CK = (55, 57)
T_CV = (57, 59)
T_DQ = (59, 67)
T_DK = (67, 69)
T_DV = (69, 71)
T_G = (71, 107)
DILS = (1, 4, 16)
NKB = tuple(1024 + 128 * d for d in DILS)
KX_A, KX_B, KX_C, KX_D, KX_PE, KX_ROWS = 0, 1024, 2560, 2816, 3072, 3136
VX_A, VX_B, VX_C, VX_D, VX_COLS = 0, 1024, 2560, 2816, 3072
GROUPS = [[0, 1, 2, 3], [4, 5, 6, 7]]
ARENA_N32 = 46848


def kxall(D, r, row0, n):
    c = row0 // 256
    o = row0 % 256
    crows = 64 if c == 12 else 256
    return D["KXall"][c, r * crows + o:r * crows + o + n, :]


def vx(D, col0, n):
    return D["VX"][col0 // 256][:, col0 % 256:col0 % 256 + n]


def vxall(D, r, t0, t1, col0, n):
    return D["VXall"][col0 // 256][r * 1024 + t0:r * 1024 + t1, col0 % 256:col0 % 256 + n]


def phase_P(P, D, C, l):
    P.phase_reset()
    x = D["x"] if l == 0 else D["x1"]
    wt = D["wt"]
    QA, QB, QC, QD, G, KX, VX = D["QA"], D["QB"], D["QC"], D["QD"], D["G"], D["KX"], D["VX"]
    ident, ones, epst = C["ident"], C["ones"], C["epst"]

    hT = P.asb("hT", [128, 32, S_LOC], BF16)
    NB = 3
    ring = [P.asb(f"wr{i}", [128, 32, 128], BF16) for i in range(NB)]
    wup = P.asb("wup", [128, 6 * 1536], BF16)
    lat = P.asb("lat", [128, 6, S_LOC], F32)
    latn = P.asb("latn", [128, 6, S_LOC], BF16)
    tcos = P.asb("tcos", [128, S_LOC], F32)
    tsin = P.asb("tsin", [128, S_LOC], F32)
    sq = [P.asb(f"sq{i}", [128, 512], BF16) for i in range(2)]
    rs = P.asb("rs", [128, S_LOC], F32)
    xf = [P.asb(f"xf{i}", [128, 512], F32) for i in range(2)]
    xs = [P.asb("xs0", [128, 512], F32)] * 2
    t1 = [P.asb(f"t1{i}", [128, 512], F32) for i in range(2)]
    t2 = [P.asb("t20", [128, 512], F32)] * 2
    NSTG = 2
    stg = [P.asb(f"stg{i}", [128, S_LOC], BF16) for i in range(NSTG)]
    vst = [P.asb(f"vst{i}", [128, 8, 128], BF16) for i in range(2)]
    ssq = P.asb("ssq", [128, 2], F32)
    ssq8 = P.asb("ssq8", [128, 8], F32)
    junk = P.asb("junk", [128, 512], F32)
    rstd = P.asb("rstd", [128, 2], F32)
    gat = P.asb("gat", [128, 32], F32)
    gqa_t = P.asb("gqa_t", [128, 6], F32)
    gkva_t = P.asb("gkva_t", [128, 4], F32)
    gqn_t = P.asb("gqn_t", [128, 2], F32)
    gkn_t = P.asb("gkn_t", [128, 2], F32)
    pj = [P.aps(f"pj{i}", [128, 512], F32) for i in range(4)]
    tp = [P.aps(f"tp{i}", [128, 8, 128], BF16) for i in range(2)]
    sp_ = [P.aps(f"ssqp{i}", [128, 512], F32) for i in range(2)]

    P.dma("sp", gat[:, :], D["gattn"][l], [], [gat])
    P.dma("sp", gqa_t[:, :], D["gqa"][l], [], [gqa_t])
    P.dma("sp", gkva_t[:, :], D["gkva"][l], [], [gkva_t])
    P.dma("sp", gqn_t[:, 0:1], D["gqn"][l], [], [gqn_t])
    P.dma("sp", gkn_t[:, 0:1], D["gkn"][l], [], [gkn_t])

    hsub = [[P.sub(hT, f"hT{tt}a"), P.sub(hT, f"hT{tt}b")] for tt in range(NT)]
    xt_ap = lat[:, 0:4, :].rearrange("p a b -> p (a b)")
    xn_ap = latn[:, 0:4, :].rearrange("p a b -> p (a b)")
    ntp = 0
    xreads = [] if l == 0 else [x]
    for tt in range(NT):
        P.dma("sp", xt_ap, x[tt * 128:(tt + 1) * 128, :], xreads, [lat])
        P.g("dve", "memset", [], [ssq8], ssq8[:, :], 0.0)
        for k in range(8):
            P.g("act", "activation", [lat, ssq8], [junk, ssq8], out=junk[:, :], in_=xt_ap[:, k * 512:(k + 1) * 512], func=AF.Square,
                accum_out=ssq8[:, k:k + 1])
        P.g("dve", "reduce_sum", [ssq8], [ssq], out=ssq[:, 0:1], in_=ssq8[:, 0:8], axis=AX.X)
        P.g("act", "activation", [ssq, epst], [rstd], out=rstd[:, 0:1], in_=ssq[:, 0:1], func=AF.Sqrt, bias=epst[:], scale=1.0 / 4096)
        P.g("dve", "reciprocal", [rstd], [rstd], out=rstd[:, 0:1], in_=rstd[:, 0:1])
        P.g("dve", "tensor_scalar", [lat, rstd], [latn], out=xn_ap, in0=xt_ap, scalar1=rstd[:, 0:1], scalar2=None, op0=ALU.mult)
        for grp in range(4):
            tpt = tp[ntp % 2]
            ntp += 1
            for j in range(8):
                c = grp * 8 + j
                P.tr(tpt[:, j, :], xn_ap[:, c * 128:(c + 1) * 128], ident[:], [latn, ident], [tpt])
            for j in range(8):
                c = grp * 8 + j
                dst = hT[:, c, tt * 128:(tt + 1) * 128]
                P.g("dve", "tensor_scalar", [tpt, gat], [hsub[tt][j % 2]], out=dst, in0=tpt[:, j, :],
                    scalar1=gat[:, c:c + 1], scalar2=None, op0=ALU.mult)

    def hreads(tc):
        r = []
        for tt in range(tc * 4, tc * 4 + 4):
            r += hsub[tt]
        return r

    state = {"k": 0, "stg": 0, "pj": 0, "vst": 0, "tp": ntp}

    def proj(widx, M, epi):
        slot = ring[state["k"] % NB]
        state["k"] += 1
        P.dma("pool", slot[:, :, :], wt[l, widx].rearrange("p (c n) -> p c n", c=32), [], [slot])
        for tc in range(2):
            ps = pj[state["pj"] % 4]
            state["pj"] += 1
            hr = hreads(tc)
            for c in range(32):
                P.mm(ps[0:M, :], slot[:, c, 0:M], hT[:, c, tc * 512:(tc + 1) * 512], c == 0, c == 31, [slot] + hr, [ps])
            epi(tc, ps)

    def upproj(src, nchunk, wcols, col0, M, epi):
        for tc in range(2):
            ps = pj[state["pj"] % 4]
            state["pj"] += 1
            for c in range(nchunk):
                P.mm(ps[0:M, :], wup[:, c * wcols + col0: c * wcols + col0 + M], src[:, c, tc * 512:(tc + 1) * 512],
                     c == 0, c == nchunk - 1, [wup, src], [ps])
            epi(tc, ps)

    def next_stg():
        s = stg[state["stg"] % NSTG]
        state["stg"] += 1
        return s

    def scopy(out, in_, scale, reads, writes):
        P.g("dve", "tensor_scalar", reads, writes, out=out, in0=in_, scalar1=float(scale), scalar2=None, op0=ALU.mult)

    def epi_plain(dst_ap, dst_tile, scale, M=128):
        s = next_stg()

        def epi(tc, ps):
            scopy(s[0:M, tc * 512:(tc + 1) * 512], ps[0:M, :], scale, [ps], [s])
            if tc == 1:
                P.dma("sp", dst_ap, s[0:M, :], [s], [dst_tile])
        return epi

    def epi_silu(dst_ap, dst_tile):
        s = next_stg()

        def epi(tc, ps):
            P.g("act", "activation", [ps], [s], out=s[:, tc * 512:(tc + 1) * 512], in_=ps[:, :], func=AF.Silu)
            if tc == 1:
                P.dma("sp", dst_ap, s[:, :], [s], [dst_tile])
        return epi

    def epi_v(col0):
        s = next_stg()

        def epi(tc, ps):
            scopy(s[:, tc * 512:(tc + 1) * 512], ps[:, :], 1.0, [ps], [s])
            if tc == 1:
                tpt = tp[state["tp"] % 2]
                state["tp"] += 1
                for b in range(8):
                    P.tr(tpt[:, b, :], s[:, b * 128:(b + 1) * 128], ident[:], [s, ident], [tpt])
                vs = vst[state["vst"] % 2]
                state["vst"] += 1
                P.g("dve", "tensor_copy", [tpt], [vs], out=vs[:, :, :], in_=tpt[:, :, :])
                P.dma("sp", vx(D, col0, 128).rearrange("(b p) d -> p b d", p=128), vs[:, :, :], [vs], [VX])
        return epi

    def epi_latent(c, n, ndim):
        def epi(tc, ps):
            sl = slice(tc * 512, (tc + 1) * 512)
            P.g("dve", "tensor_copy", [ps], [lat], out=lat[:, c, sl], in_=ps[:, :])
            sqt = sq[tc]
            P.g("act", "activation", [lat], [sqt], out=sqt[:, :], in_=lat[:, c, sl], func=AF.Square)
            P.mm(sp_[tc][:, :], ones[:, :], sqt[:, :], c == 0, c == n - 1, [ones, sqt], [sp_[tc]])
        return epi

    def latent_norm(n, ndim, gt):
        for tc in range(2):
            sl = slice(tc * 512, (tc + 1) * 512)
            P.g("act", "activation", [sp_[tc], epst], [rs], out=rs[:, sl], in_=sp_[tc][:, :], func=AF.Sqrt, bias=epst[:],
                scale=1.0 / ndim)
            P.g("dve", "reciprocal", [rs], [rs], out=rs[:, sl], in_=rs[:, sl])
            for c in range(n):
                P.g("dve", "tensor_scalar", [lat, gt], [lat], out=lat[:, c, sl], in0=lat[:, c, sl], scalar1=gt[:, c:c + 1], scalar2=None, op0=ALU.mult)
                P.g("dve", "tensor_tensor", [lat, rs], [latn], out=latn[:, c, sl], in0=lat[:, c, sl], in1=rs[:, sl], op=ALU.mult)

    def rope_tail(tc, src, Pn, s):
        sl = slice(tc * 512, (tc + 1) * 512)
        h = Pn // 2
        xst = xs[tc]
        P.g("dve", "tensor_copy", [src], [xst], out=xst[0:h, :], in_=src[h:Pn, :])
        P.g("dve", "tensor_copy", [src], [xst], out=xst[h:Pn, :], in_=src[0:h, :])
        P.g("dve", "tensor_tensor", [src, tcos], [t1[tc]], out=t1[tc][0:Pn, :], in0=src[0:Pn, :], in1=tcos[0:Pn, sl], op=ALU.mult)
        P.g("dve", "tensor_tensor", [xst, tsin], [t2[tc]], out=t2[tc][0:Pn, :], in0=xst[0:Pn, :], in1=tsin[0:Pn, sl], op=ALU.mult)
        P.g("dve", "tensor_tensor", [t1[tc], t2[tc]], [s], out=s[0:Pn, sl], in0=t1[tc][0:Pn, :], in1=t2[tc][0:Pn, :], op=ALU.add)

    def epi_rope64(dst_ap, dst_tile, scale):
        s = next_stg()

        def epi(tc, ps):
            scopy(xf[tc][0:64, :], ps[0:64, :], scale, [ps], [xf[tc]])
            rope_tail(tc, xf[tc], 64, s)
            if tc == 1:
                P.dma("sp", dst_ap, s[0:64, :], [s], [dst_tile])
        return epi

    def epi_headnorm_rope(dst_ap, dst_tile, scale, gt):
        s = next_stg()

        def epi(tc, ps):
            sqt = sq[tc]
            scopy(t1[tc][:, :], ps[:, :], scale, [ps], [t1[tc]])
            P.g("act", "activation", [t1[tc]], [sqt], out=sqt[:, :], in_=t1[tc][:, :], func=AF.Square, scale=float(1.0 / scale))
            P.mm(sp_[tc][:, :], ones[:, :], sqt[:, :], True, True, [ones, sqt], [sp_[tc]])
            sl = slice(tc * 512, (tc + 1) * 512)
            P.g("act", "activation", [sp_[tc], epst], [rs], out=rs[:, sl], in_=sp_[tc][:, :], func=AF.Sqrt, bias=epst[:], scale=1.0 / 128)
            P.g("dve", "reciprocal", [rs], [rs], out=rs[:, sl], in_=rs[:, sl])
            P.g("dve", "tensor_scalar", [t1[tc], gt], [t1[tc]], out=t1[tc][:, :], in0=t1[tc][:, :], scalar1=gt[:, 0:1], scalar2=None, op0=ALU.mult)
            P.g("dve", "tensor_tensor", [t1[tc], rs], [xf[tc]], out=xf[tc][:, :], in0=t1[tc][:, :], in1=rs[:, sl], op=ALU.mult)
            rope_tail(tc, xf[tc], 128, s)
            if tc == 1:
                P.dma("sp", dst_ap, s[:, :], [s], [dst_tile])
        return epi

    sA = 192 ** -0.5
    sB = 128 ** -0.5
    P.dma("sp", tcos[0:64, :], D["cosA"][:, :], [], [tcos])
    P.dma("sp", tsin[0:64, :], D["sinA"][:, :], [], [tsin])
    P.dma("pool", wup[:, 0:4 * 2048], D["wukv"][l], [], [wup])
    for c in range(4):
        proj(T_CKV[0] + c, 128, epi_latent(c, 4, 512))
    latent_norm(4, 512, gkva_t)
    for h in range(8):
        upproj(latn, 4, 2048, h * 256, 128, epi_plain(KX[KX_A + h * 128:KX_A + (h + 1) * 128, :], KX, 1.0))
        upproj(latn, 4, 2048, h * 256 + 128, 128, epi_v(VX_A + h * 128))
    proj(T_KPE, 64, epi_rope64(KX[KX_PE:KX_PE + 64, :], KX, 1.0))
    for i in range(12):
        proj(T_BK[0] + i, 128, epi_plain(KX[KX_B + i * 128:KX_B + (i + 1) * 128, :], KX, 1.0))
    for i in range(12):
        proj(T_BV[0] + i, 128, epi_v(VX_B + i * 128))
    for i in range(2):
        proj(T_DK[0] + i, 128, epi_plain(KX[KX_D + i * 128:KX_D + (i + 1) * 128, :], KX, 1.0))
    for i in range(2):
        proj(T_DV[0] + i, 128, epi_v(VX_D + i * 128))
    for i in range(2):
        proj(T_CV[0] + i, 128, epi_v(VX_C + i * 128))
    P.dma("pool", wup[:, :], D["wuq"][l], [], [wup])
    for c in range(6):
        proj(T_CQ[0] + c, 128, epi_latent(c, 6, 768))
    latent_norm(6, 768, gqa_t)
    for h in range(8):
        upproj(latn, 6, 1536, h * 192, 128, epi_plain(QA[h, 0:128, :], QA, sA))
        upproj(latn, 6, 1536, h * 192 + 128, 64, epi_rope64(QA[h, 128:192, :], QA, sA))
    P.dma("sp", tcos[:, :], D["cosC"][:, :], [], [tcos])
    P.dma("sp", tsin[:, :], D["sinC"][:, :], [], [tsin])
    for i in range(2):
        proj(T_CK[0] + i, 128, epi_headnorm_rope(KX[KX_C + i * 128:KX_C + (i + 1) * 128, :], KX, 1.0, gkn_t))
    for c in range(13):
        rows = 64 if c == 12 else 256
        P.coll("agk", KX[c * 256:c * 256 + rows, :], D["KXall"][c, 0:4 * rows, :], GROUPS, [KX], [D["KXall"]])
    for c in range(12):
        P.coll("agv", VX[c], D["VXall"][c], GROUPS, [VX], [D["VXall"]])
    for i in range(8):
        proj(T_CQh[0] + i, 128, epi_headnorm_rope(QC[i, :, :], QC, sB, gqn_t))
    for i in range(12):
        proj(T_BQ[0] + i, 128, epi_plain(QB[i, :, :], QB, sB))
    for i in range(8):
        proj(T_DQ[0] + i, 128, epi_plain(QD[i, :, :], QD, sB))
    for i in range(36):
        proj(T_G[0] + i, 128, epi_silu(G[i, :, :], G))


def phase_T1(P, D, C, l):
    P.phase_reset()
    QA, QB, QC, QD, G, KX, VX = D["QA"], D["QB"], D["QC"], D["QD"], D["G"], D["KX"], D["VX"]
    KXall, VXall, MT, VBext = D["KXall"], D["VXall"], D["MT"], D["VBext"]
    ones, onesL, onesR, selM = C["ones"], C["onesL"], C["onesR"], C["selM"]
    MD, MB = C["MD"], C["MB"]

    q0 = [P.asb(f"q0{i}", [128, S_LOC], BF16) for i in range(2)]
    q1 = [P.asb(f"q1{i}", [128, S_LOC], BF16) for i in range(2)]
    kn = [P.asb(f"kn{i}", [128, 4096], BF16) for i in range(2)]
    kpe = P.asb("kpe", [128, 4096], BF16)
    vv = [P.asb(f"vv{i}", [128, 32, 128], BF16) for i in range(2)]
    pts = [P.asb(f"pt{i}", [128, 512], BF16) for i in range(3)]
    gts = [P.asb(f"gt{i}", [128, S_LOC], BF16) for i in range(2)]
    rinv = [P.asb(f"rinv{i}", [128, 512], F32) for i in range(2)]
    otmp = [P.asb(f"otmp{i}", [128, 512], F32) for i in range(2)]
    ost = [P.asb(f"ost{i}", [128, S_LOC], BF16) for i in range(2)]
    esk = P.asb("esk", [128, 8], F32)
    qd4 = P.asb("qd4", [128, 4, S_LOC], BF16)
    gt4 = P.asb("gt4", [128, 4, S_LOC], BF16)
    ost4 = P.asb("ost4", [128, 4, S_LOC], BF16)
    kdh = P.asb("kdh", [128, 1280], BF16)
    vdh = P.asb("vdh", [128, 10, 128], BF16)
    tmpd = [P.asb(f"tmpd{i}", [128, 512], F32) for i in range(3)]
    ptd = [P.asb(f"ptd{i}", [128, 512], BF16) for i in range(3)]
    kb_t = P.asb("kb_t", [128, 3072], BF16)
    vb_t = P.asb("vb_t", [128, 48, 128], BF16)
    qb_t = P.asb("qb_t", [128, S_LOC], BF16)
    num = [P.asb(f"num{i}", [128, S_LOC], F32) for i in range(3)]
    Ls = [P.asb(f"Ls{i}", [128, S_LOC], F32) for i in range(3)]
    ltot = P.asb("ltot", [128, S_LOC], F32)
    tmpb = P.asb("tmpb", [128, 192], F32)
    ptb = P.asb("ptb", [128, 192], BF16)
    cand = [P.asb(f"cand{i}", [128, 512], BF16) for i in range(4)]
    selo = [P.asb(f"selo{i}", [128, 512], BF16) for i in range(2)]

    sts = [P.aps(f"st{i}", [128, 512], F32) for i in range(3)]
    Ops = [P.aps(f"Op{i}", [128, 512], F32) for i in range(2)]
    Lps = [P.aps(f"Lp{i}", [128, 512], F32) for i in range(2)]
    selp = P.aps("selp", [128, 512], F32)

    P.dma("sp", esk[:, :], D["sinksb"][l], [], [esk])
    P.g("act", "activation", [esk], [esk], out=esk[:, :], in_=esk[:, :], func=AF.Exp)

    st = {"h": 0, "ol": 0, "fin": 0, "sel": 0}

    def select(side, srcs, npart, ncols, dst_ap, dst_tile):
        for r in range(4):
            P.dma("sp", cand[r][0:npart, 0:ncols], srcs[r], [KXall, VXall], [cand[r]])
        for r in range(4):
            P.mm(selp[0:npart, 0:ncols], selM[0:npart, (side * 4 + r) * 128:(side * 4 + r) * 128 + npart], cand[r][0:npart, 0:ncols],
                 r == 0, r == 3, [selM, cand[r]], [selp])
        P.g("dve", "tensor_copy", [selp], [dst_tile], out=dst_ap, in_=selp[0:npart, 0:ncols])

    def finalize(Ot, Lt, esk_cols=None):
        k = st["fin"] % 2
        st["fin"] += 1
        ri = rinv[k]
        if esk_cols is None:
            P.g("dve", "reciprocal", [Lt], [ri], out=ri[:, :], in_=Lt[:, :])
        else:
            for i, hc in enumerate(esk_cols):
                P.g("dve", "tensor_scalar", [Lt, esk], [ri], out=ri[:, i * 128:(i + 1) * 128], in0=Lt[:, i * 128:(i + 1) * 128],
                    scalar1=esk[:, hc:hc + 1], scalar2=None, op0=ALU.add)
            P.g("dve", "reciprocal", [ri], [ri], out=ri[:, :], in_=ri[:, :])
        P.g("dve", "tensor_tensor", [Ot, ri], [otmp[k]], out=otmp[k][:, :], in0=Ot[:, :], in1=ri[:, :], op=ALU.mult)
        return otmp[k]

    def dense_head(qparts, kparts, vt, gt, mt_idx):
        os_ = ost[st["h"] % 2]
        st["h"] += 1
        for qc in range(2):
            Ot = Ops[st["ol"] % 2]
            Lt = Lps[st["ol"] % 2]
            st["ol"] += 1

            def qk(kb):
                s_ = sts[kb % 2]
                n = len(qparts)
                for i in range(n):
                    qt, kp = qparts[i]
                    kt = kparts[i]
                    P.mm(s_[:, :], kt[0:kp, kb * 128:(kb + 1) * 128], qt[0:kp, qc * 512:(qc + 1) * 512], i == 0, i == n - 1,
                         [kt, qt], [s_])
            qk(0)
            for kb in range(32):
                if kb + 1 < 32:
                    qk(kb + 1)
                s_ = sts[kb % 2]
                pt = pts[kb % 3]
                P.g("act", "activation", [s_], [pt], out=pt[:, :], in_=s_[:, :], func=AF.Exp)
                P.mm(Ot[:, :], vt[:, kb, :], pt[:, :], kb == 0, kb == 31, [vt, pt], [Ot])
                P.mm(Lt[:, :], ones[:, :], pt[:, :], kb == 0, kb == 31, [ones, pt], [Lt])
            o = finalize(Ot, Lt)
            P.g("dve", "tensor_tensor", [o, gt], [os_], out=os_[:, qc * 512:(qc + 1) * 512], in0=o[:, :],
                in1=gt[:, qc * 512:(qc + 1) * 512], op=ALU.mult)
        P.dma("sp", MT[mt_idx, :, :], os_[:, :], [os_], [MT])

    def load_k_all(dst, npart, row0):
        for r in range(4):
            P.dma("sp", dst[0:npart, r * 1024:(r + 1) * 1024], kxall(D, r, row0, npart), [KXall], [dst])

    def load_v_all(dst, col0):
        for r in range(4):
            P.dma("sp", dst[:, r * 8:(r + 1) * 8, :],
                  vxall(D, r, 0, 1024, col0, 128).rearrange("(b p) d -> p b d", p=128), [VXall], [dst])

    load_k_all(kpe, 64, KX_PE)
    for h in range(8):
        b = h % 2
        P.dma("sp", q0[b][:, :], QA[h, 0:128, :], [QA], [q0[b]])
        P.dma("sp", q1[b][0:64, :], QA[h, 128:192, :], [QA], [q1[b]])
        load_k_all(kn[b], 128, KX_A + h * 128)
        load_v_all(vv[b], VX_A + h * 128)
        P.dma("sp", gts[b][:, :], G[h, :, :], [G], [gts[b]])
        dense_head([(q0[b], 128), (q1[b], 64)], [kn[b], kpe], vv[b], gts[b], h)
    for g in range(2):
        load_k_all(kn[g], 128, KX_C + g * 128)
        load_v_all(vv[g], VX_C + g * 128)
        for i in range(4):
            hq = g * 4 + i
            b = hq % 2
            P.dma("sp", q0[b][:, :], QC[hq, :, :], [QC], [q0[b]])
            P.dma("sp", gts[b][:, :], G[20 + hq, :, :], [G], [gts[b]])
            dense_head([(q0[b], 128)], [kn[g]], vv[g], gts[b], 20 + hq)
    for g in range(2):
        for i in range(4):
            P.dma("sp", qd4[:, i, :], QD[g * 4 + i, :, :], [QD], [qd4])
            P.dma("sp", gt4[:, i, :], G[28 + g * 4 + i, :, :], [G], [gt4])
        krow = KX_D + g * 128
        vcol = VX_D + g * 128
        P.dma("sp", kdh[:, 128:1152], KX[krow:krow + 128, :], [KX], [kdh])
        P.dma("sp", vdh[:, 1:9, :], vx(D, vcol, 128).rearrange("(b p) d -> p b d", p=128), [VX], [vdh])
        select(0, [kxall(D, r, krow, 128)[:, 896:1024] for r in range(4)], 128, 128, kdh[:, 0:128], kdh)
        select(1, [kxall(D, r, krow, 128)[:, 0:128] for r in range(4)], 128, 128, kdh[:, 1152:1280], kdh)
        select(0, [vxall(D, r, 896, 1024, vcol, 128) for r in range(4)], 128, 128, vdh[:, 0, :], vdh)
        select(1, [vxall(D, r, 0, 128, vcol, 128) for r in range(4)], 128, 128, vdh[:, 9, :], vdh)
        for qb in range(8):
            Ot = Ops[st["ol"] % 2]
            Lt = Lps[st["ol"] % 2]
            st["ol"] += 1
            qsl = slice(qb * 128, (qb + 1) * 128)
            for j in range(3):
                P.mm(sts[j][:, :], kdh[:, (qb + j) * 128:(qb + j + 1) * 128], qd4[:, :, qsl], True, True, [kdh, qd4], [sts[j]])
                off = (g * 3 + j) * 512
                P.g("dve", "tensor_tensor", [sts[j], MD], [tmpd[j]], out=tmpd[j][:, :], in0=sts[j][:, :], in1=MD[:, off:off + 512], op=ALU.add)
                P.g("act", "activation", [tmpd[j]], [ptd[j]], out=ptd[j][:, :], in_=tmpd[j][:, :], func=AF.Exp)
            for j in range(3):
                kbi = qb + j
                on = onesL if kbi == 0 else (onesR if kbi == 9 else ones)
                P.mm(Ot[:, :], vdh[:, kbi, :], ptd[j][:, :], j == 0, j == 2, [vdh, ptd[j]], [Ot])
                P.mm(Lt[:, :], on[:, :], ptd[j][:, :], j == 0, j == 2, [on, ptd[j]], [Lt])
            o = finalize(Ot, Lt, esk_cols=[g * 4 + i for i in range(4)])
            P.g("dve", "tensor_tensor", [o, gt4], [ost4], out=ost4[:, :, qsl], in0=o[:, :].rearrange("p (a b) -> p a b", a=4),
                in1=gt4[:, :, qsl], op=ALU.mult)
        for i in range(4):
            P.dma("sp", MT[28 + g * 4 + i, :, :], ost4[:, i, :], [ost4], [MT])
    for p in range(3):
        d = DILS[p]
        hl = 64 * d
        vc0 = VX_B + p * 512
        for q in range(2):
            P.dma("sp", VBext[p][hl:hl + 1024, q * 256:(q + 1) * 256], vx(D, vc0 + q * 256, 256), [VX], [VBext[p]], semtile=VBext[p])
        for side in range(2):
            t0 = 0
            while t0 < hl:
                nt = min(128, hl - t0)
                so = selo[st["sel"] % 2]
                st["sel"] += 1
                for q in range(2):
                    if side == 0:
                        srcs = [vxall(D, r, 1024 - hl + t0, 1024 - hl + t0 + nt, vc0 + q * 256, 256) for r in range(4)]
                    else:
                        srcs = [vxall(D, r, t0, t0 + nt, vc0 + q * 256, 256) for r in range(4)]
                    select(side, srcs, nt, 256, so[0:nt, q * 256:(q + 1) * 256], so)
                drow = t0 if side == 0 else hl + 1024 + t0
                P.dma("sp", VBext[p][drow:drow + nt, :], so[0:nt, 0:512], [so], [VBext[p]])
                t0 += nt
    stb = sts[0]
    for hh in range(4):
        for p in range(3):
            dil = DILS[p]
            hl = 64 * dil
            nq = 1024 // dil
            nqc = nq // 64
            nkc = nqc + 2
            hidx = p * 4 + hh
            krow = KX_B + hidx * 128
            P.dma("sp", qb_t[:, :], QB[hidx, :, :], [QB], [qb_t])
            P.dma("sp", kb_t[:, hl:hl + 1024], KX[krow:krow + 128, :], [KX], [kb_t])
            c0 = 0
            while c0 < hl:
                ncol = min(512, hl - c0)
                select(0, [kxall(D, r, krow, 128)[:, 1024 - hl + c0:1024 - hl + c0 + ncol] for r in range(4)],
                       128, ncol, kb_t[:, c0:c0 + ncol], kb_t)
                select(1, [kxall(D, r, krow, 128)[:, c0:c0 + ncol] for r in range(4)],
                       128, ncol, kb_t[:, hl + 1024 + c0:hl + 1024 + c0 + ncol], kb_t)
                c0 += ncol
            vsrc = VBext[p][:, hh * 128:(hh + 1) * 128].rearrange("(c k r) dv -> r k c dv", k=64, r=dil)
            for r in range(dil):
                P.dma("sp", vb_t[0:64, r * nkc:(r + 1) * nkc, :], vsrc[r], [VBext[p]], [vb_t])
            qi = 0
            bank = []
            for r in range(dil):
                for i in range(nqc):
                    if qi % 8 == 0:
                        Ot = Ops[st["ol"] % 2]
                        Lt = Lps[st["ol"] % 2]
                        st["ol"] += 1
                        bank = []
                    s0 = r + dil * 64 * i
                    qsl = slice(s0, s0 + dil * 63 + 1, dil)
                    for j in range(3):
                        k0 = r + dil * 64 * (i + j)
                        P.mm(stb[0:64, j * 64:(j + 1) * 64], kb_t[:, k0:k0 + dil * 63 + 1:dil], qb_t[:, qsl], True, True,
                             [kb_t, qb_t], [stb])
                    P.g("dve", "tensor_tensor", [stb, MB], [tmpb], out=tmpb[0:64, :], in0=stb[0:64, 0:192],
                        in1=MB[0:64, hidx * 192:(hidx + 1) * 192], op=ALU.add)
                    P.g("act", "activation", [tmpb], [ptb], out=ptb[0:64, :], in_=tmpb[0:64, :], func=AF.Exp)
                    col = (qi % 8) * 64
                    kc0 = r * nkc + i
                    for j in range(3):
                        on = onesL if (i + j == 0) else (onesR if (i + j == nkc - 1) else ones)
                        P.mm(Ot[:, col:col + 64], vb_t[0:64, kc0 + j, :], ptb[0:64, j * 64:(j + 1) * 64], j == 0, j == 2, [vb_t, ptb], [Ot])
                        P.mm(Lt[:, col:col + 64], on[0:64, :], ptb[0:64, j * 64:(j + 1) * 64], j == 0, j == 2, [on, ptb], [Lt])
                    bank.append((col, qsl))
                    qi += 1
                    if qi % 8 == 0:
                        for (cc, qs) in bank:
                            P.g("dve", "tensor_copy", [Ot], [num[p]], out=num[p][:, qs], in_=Ot[:, cc:cc + 64])
                            P.g("dve", "tensor_copy", [Lt], [Ls[p]], out=Ls[p][:, qs], in_=Lt[:, cc:cc + 64])
        P.g("dve", "tensor_tensor", [Ls[0], Ls[1]], [ltot], out=ltot[:, :], in0=Ls[0][:, :], in1=Ls[1][:, :], op=ALU.add)
        P.g("dve", "tensor_tensor", [ltot, Ls[2]], [ltot], out=ltot[:, :], in0=ltot[:, :], in1=Ls[2][:, :], op=ALU.add)
        P.g("dve", "reciprocal", [ltot], [ltot], out=ltot[:, :], in_=ltot[:, :])
        for p in range(3):
            b = p % 2
            P.dma("sp", gts[b][:, :], G[8 + p * 4 + hh, :, :], [G], [gts[b]])
            P.g("dve", "tensor_tensor", [num[p], ltot], [num[p]], out=num[p][:, :], in0=num[p][:, :], in1=ltot[:, :], op=ALU.mult)
            os_ = ost[b]
            P.g("dve", "tensor_tensor", [num[p], gts[b]], [os_], out=os_[:, :], in0=num[p][:, :], in1=gts[b][:, :], op=ALU.mult)
            P.dma("sp", MT[8 + p * 4 + hh, :, :], os_[:, :], [os_], [MT])


def phase_T2(P, D, C, l, final):
    P.phase_reset()
    MT = D["MT"]
    x = D["x"] if l == 0 else D["x1"]
    wo = D["wo"]
    epst = C["epst"]
    dst = D["xsc"] if final else D["x1"]
    xo = D["xo"]
    mt = P.asb("mt", [128, 36, S_LOC], BF16)
    wob = [P.asb(f"wob{i}", [128, 36, 256], BF16) for i in range(2)]
    xt = [P.asb(f"xt{i}", [128, 256], F32) for i in range(3)]
    yt = [P.asb(f"yt{i}", [128, 256], F32) for i in range(3)]
    ps = [P.aps(f"ps{i}", [128, 512], F32) for i in range(4)]
    mts = [P.sub(mt, f"mt{c}") for c in range(36)]
    for c in range(36):
        P.dma("sp", mt[:, c, :], MT[c, :, :], [MT], [mts[c]], semtile=mt)
    n = 0
    xr = [] if l == 0 else [x]
    for dc in range(16):
        wb = wob[dc % 2]
        P.dma("pool", wb[:, :, :], wo[l, dc].rearrange("p (c n) -> p c n", c=36), [], [wb])
        for tt in range(8):
            pt = ps[n % 4]
            xt_ = xt[n % 3]
            yt_ = yt[n % 3]
            n += 1
            P.dma("sp", xt_[:, :], x[tt * 128:(tt + 1) * 128, dc * 256:(dc + 1) * 256], xr, [xt_])
            for c in range(36):
                P.mm(pt[:, 0:256], mt[:, c, tt * 128:(tt + 1) * 128], wb[:, c, :], c == 0, c == 35, [mts[c], wb], [pt])
            P.g("dve", "tensor_tensor", [pt, xt_], [yt_], out=yt_[:, :], in0=pt[:, 0:256], in1=xt_[:, :], op=ALU.add)
            P.dma("sp", dst[tt * 128:(tt + 1) * 128, dc * 256:(dc + 1) * 256], yt_[:, :], [yt_], [dst])
    if final:
        xf = P.asb("xff", [128, 4096], F32)
        gf = P.asb("gf", [128, 4096], F32)
        of = P.asb("of", [128, 4096], F32)
        junk = P.asb("junk2", [128, 512], F32)
        ssq8 = P.asb("ssq8b", [128, 8], F32)
        ssq = P.asb("ssqb", [128, 2], F32)
        P.dma("sp", gf[:, :], D["gfin"][:, :], [], [gf])
        for tt in range(8):
            P.dma("sp", xf[:, :], dst[tt * 128:(tt + 1) * 128, :], [dst], [xf])
            P.g("dve", "memset", [], [ssq8], ssq8[:, :], 0.0)
            for k in range(8):
                P.g("act", "activation", [xf, ssq8], [junk, ssq8], out=junk[:, :], in_=xf[:, k * 512:(k + 1) * 512], func=AF.Square,
                    accum_out=ssq8[:, k:k + 1])
            P.g("dve", "reduce_sum", [ssq8], [ssq], out=ssq[:, 0:1], in_=ssq8[:, 0:8], axis=AX.X)
            P.g("act", "activation", [ssq, epst], [ssq], out=ssq[:, 0:1], in_=ssq[:, 0:1], func=AF.Sqrt, bias=epst[:], scale=1.0 / 4096)
            P.g("dve", "reciprocal", [ssq], [ssq], out=ssq[:, 0:1], in_=ssq[:, 0:1])
            P.g("dve", "tensor_scalar", [xf, ssq], [xf], out=xf[:, :], in0=xf[:, :], scalar1=ssq[:, 0:1], scalar2=None, op0=ALU.mult)
            P.g("dve", "tensor_tensor", [xf, gf], [of], out=of[:, :], in0=xf[:, :], in1=gf[:, :], op=ALU.mult)
            P.dma("sp", xo[tt * 128:(tt + 1) * 128, :], of[:, :], [of], [])


def build_fused(nlayers=2, phases="PAT", final_last=True):
    nc = bass.Bass("TRN2", target_bir_lowering=False)
    P = Prog(nc)
    IN = "ExternalInput"
    D = {}
    D["x"] = P.dram("x", [S_LOC, 4096], F32, IN)
    D["wt"] = P.dram("wt", [nlayers, NW, 128, 4096], F32, IN)
    D["wo"] = P.dram("wo", [nlayers, 16, 128, 36 * 256], F32, IN)
    D["gattn"] = P.dram("gattn", [nlayers, 128, 32], F32, IN)
    D["gqa"] = P.dram("gqa", [nlayers, 128, 6], F32, IN)
    D["gkva"] = P.dram("gkva", [nlayers, 128, 4], F32, IN)
    D["gqn"] = P.dram("gqn", [nlayers, 128, 1], F32, IN)
    D["gkn"] = P.dram("gkn", [nlayers, 128, 1], F32, IN)
    D["wuq"] = P.dram("wuq", [nlayers, 128, 6 * 1536], F32, IN)
    D["wukv"] = P.dram("wukv", [nlayers, 128, 4 * 2048], F32, IN)
    D["cosA"] = P.dram("cosA", [64, S_LOC], F32, IN)
    D["sinA"] = P.dram("sinA", [64, S_LOC], F32, IN)
    D["cosC"] = P.dram("cosC", [128, S_LOC], F32, IN)
    D["sinC"] = P.dram("sinC", [128, S_LOC], F32, IN)
    D["sinksb"] = P.dram("sinksb", [nlayers, 128, 8], F32, IN)
    D["gfin"] = P.dram("gfin", [128, 4096], F32, IN)
    MDb = P.dram("MDb", [128, 2 * 3 * 512], F32, IN)
    MDm = P.dram("MDm", [128, 2 * 3 * 512], F32, IN)
    MBb = P.dram("MBb", [64, 12 * 192], F32, IN)
    MBm = P.dram("MBm", [64, 12 * 192], F32, IN)
    onesL_d = P.dram("onesL", [128, 128], F32, IN)
    onesR_d = P.dram("onesR", [128, 128], F32, IN)
    sel_d = P.dram("selM", [128, 8 * 128], F32, IN)
    D["xo"] = P.dram("xo", [S_LOC, 4096], F32, "ExternalOutput")
    D["QA"] = P.dram("QA", [8, 192, S_LOC], BF16)
    D["QB"] = P.dram("QB", [12, 128, S_LOC], BF16)
    D["QC"] = P.dram("QC", [8, 128, S_LOC], BF16)
    D["QD"] = P.dram("QD", [8, 128, S_LOC], BF16)
    D["G"] = P.dram("G", [36, 128, S_LOC], BF16)
    D["KX"] = P.dram("KX", [KX_ROWS, S_LOC], BF16)
    D["VX"] = P.dram("VX", [12, S_LOC, 256], BF16)
    D["KXall"] = P.dram("KXall", [13, 1024, S_LOC], BF16)
    D["VXall"] = P.dram("VXall", [12, 4 * S_LOC, 256], BF16)
    D["VBext"] = [P.dram(f"VBext{p}", [NKB[p], 512], BF16) for p in range(3)]
    D["MT"] = P.dram("MT", [36, 128, S_LOC], BF16)
    D["x1"] = P.dram("x1", [S_LOC, 4096], F32)
    D["xsc"] = P.dram("xsc", [S_LOC, 4096], F32)
    if nlayers == 1 and not final_last:
        D["xo"].multi = True
        D["x1"] = D["xo"]

    C = {}
    C["ident"] = P.sbuf("ident", [128, 128], BF16)
    identf = P.sbuf("identf", [128, 128], F32)
    C["ones"] = P.sbuf("ones", [128, 128], BF16)
    C["onesL"] = P.sbuf("onesLs", [128, 128], BF16)
    C["onesR"] = P.sbuf("onesRs", [128, 128], BF16)
    C["selM"] = P.sbuf("selMs", [128, 8 * 128], BF16)
    C["epst"] = P.sbuf("epst", [128, 1], F32)
    C["MD"] = P.sbuf("MD", [128, 2 * 3 * 512], F32)
    C["MB"] = P.sbuf("MB", [128, 12 * 192], F32)
    P.init_arena(ARENA_N32)
    P.g("pool", "memset", [], [identf], identf[:], 1.0)
    P.g("pool", "affine_select", [identf], [identf], out=identf[:], in_=identf[:], pattern=[[-1, 128]],
        compare_op=ALU.is_equal, fill=0.0, base=0, channel_multiplier=1)
    P.g("dve", "tensor_copy", [identf], [C["ident"]], out=C["ident"][:], in_=identf[:])
    P.g("pool", "memset", [], [C["ones"]], C["ones"][:], 1.0)
    P.g("pool", "memset", [], [C["epst"]], C["epst"][:], EPS)
    P.dma("pool", C["onesL"][:], onesL_d[:], [], [C["onesL"]])
    P.dma("pool", C["onesR"][:], onesR_d[:], [], [C["onesR"]])
    P.dma("pool", C["selM"][:], sel_d[:], [], [C["selM"]])
    P.phase_reset()
    md2 = P.asb("md2", [128, 2 * 3 * 512], F32)
    mb2 = P.asb("mb2", [128, 12 * 192], F32)
    P.dma("sp", C["MD"][:], MDb[:], [], [C["MD"]])
    P.dma("sp", md2[:, :], MDm[:], [], [md2])
    P.g("dve", "tensor_tensor", [C["MD"], md2], [C["MD"]], out=C["MD"][:], in0=C["MD"][:], in1=md2[:, :], op=ALU.add)
    P.dma("sp", C["MB"][0:64, :], MBb[:], [], [C["MB"]])
    P.dma("sp", mb2[0:64, :], MBm[:], [], [mb2])
    P.g("dve", "tensor_tensor", [C["MB"], mb2], [C["MB"]], out=C["MB"][0:64, :], in0=C["MB"][0:64, :], in1=mb2[0:64, :], op=ALU.add)

    for l in range(nlayers):
        if "P" in phases:
            phase_P(P, D, C, l)
        if "A" in phases:
            phase_T1(P, D, C, l)
        if "T" in phases:
            phase_T2(P, D, C, l, final_last and (l == nlayers - 1))
    P.emit()
    P.close()
    return nc


BF = ml_dtypes.bfloat16
S = 4096
PERM_C = np.concatenate([np.arange(0, 32), np.arange(64, 96), np.arange(32, 64), np.arange(96, 128)])


def rope_angles(pos, dim):
    inv = (10000.0 ** (-np.arange(0, dim, 2, dtype=np.float32) / dim)).astype(np.float32)
    return pos.astype(np.float32)[:, None] * inv[None, :]


def rope_tables(j):
    pos = np.arange(j * S_LOC, (j + 1) * S_LOC)
    at = rope_angles(pos, 64)
    ar = rope_angles(pos // 64, 64)
    ac = rope_angles(pos % 64, 64)
    cosA = np.concatenate([np.cos(at), np.cos(at)], 1).T
    sinA = np.concatenate([-np.sin(at), np.sin(at)], 1).T
    cosC = np.concatenate([np.cos(ar), np.cos(ac), np.cos(ar), np.cos(ac)], 1).T
    sinC = np.concatenate([-np.sin(ar), -np.sin(ac), np.sin(ar), np.sin(ac)], 1).T
    f = lambda a: np.ascontiguousarray(a, dtype=np.float32)
    return f(cosA), f(sinA), f(cosC), f(sinC)


def layer_weights_P(inp, l):
    w_in = inp["w_in"][l]
    cols = []
    for i in range(10):
        cols.append(np.arange(i * 128, (i + 1) * 128))
    tiles = np.zeros((107, 128, 32, 128), np.float32)
    w3 = w_in.reshape(32, 128, 13632)
    for i in range(10):
        tiles[i] = w3[:, :, i * 128:(i + 1) * 128].transpose(1, 0, 2)
    tiles[10, :, :, 0:64] = w3[:, :, 1280:1344].transpose(1, 0, 2)
    for t in range(11, 107):
        c0 = 1344 + (t - 11) * 128
        blk = w3[:, :, c0:c0 + 128].transpose(1, 0, 2)
        if 47 <= t < 57:
            blk = blk[:, :, PERM_C]
        tiles[t] = blk
    d = {}
    d["wt"] = tiles.reshape(107, 128, 4096)
    d["gattn"] = np.ascontiguousarray(inp["g_attn"][l].reshape(32, 128).T)
    d["gqa"] = np.ascontiguousarray(inp["g_qa"][l].reshape(6, 128).T)
    d["gkva"] = np.ascontiguousarray(inp["g_kva"][l].reshape(4, 128).T)
    d["gqn"] = np.ascontiguousarray(inp["g_qn"][l][PERM_C].reshape(128, 1))
    d["gkn"] = np.ascontiguousarray(inp["g_kn"][l][PERM_C].reshape(128, 1))
    d["wuq"] = np.ascontiguousarray(inp["w_uq"][l].reshape(6, 128, 1536).transpose(1, 0, 2).reshape(128, 6 * 1536))
    d["wukv"] = np.ascontiguousarray(inp["w_ukv"][l].reshape(4, 128, 2048).transpose(1, 0, 2).reshape(128, 4 * 2048))
    return d


def in_maps_P(inp, l, xcur):
    wd = layer_weights_P(inp, l)
    maps = []
    for c in range(8):
        b, j = divmod(c, 4)
        m = dict(wd)
        m["x"] = np.ascontiguousarray(xcur[b, j * S_LOC:(j + 1) * S_LOC, :])
        m["cosA"], m["sinA"], m["cosC"], m["sinC"] = rope_tables(j)
        maps.append(m)
    return maps


def rel_bucket_np(rel):
    nb = 16
    me = 8
    ret = np.where(rel > 0, nb, 0)
    n = np.abs(rel)
    nf = np.maximum(n, 1).astype(np.float32)
    large = me + (np.log(nf / np.float32(me)) / np.float32(math.log(1024 / me)) * np.float32(nb - me)).astype(np.int32)
    large = np.minimum(large, nb - 1)
    return ret + np.where(n < me, n, large)


def bias_tables(rel_bias):
    k = np.arange(128)[:, None]
    q = np.arange(128)[None, :]
    MDb = np.zeros((128, 2, 3, 4, 128), np.float32)
    MDm = np.zeros((128, 2, 3, 4, 128), np.float32)
    for jj in range(3):
        rel = (jj - 1) * 128 + k - q
        bk = rel_bucket_np(rel)
        ok = np.abs(rel) <= 128
        for g in range(2):
            for i in range(4):
                MDb[:, g, jj, i, :] = np.where(ok, rel_bias[bk, 12 + g * 4 + i], 0.0)
                MDm[:, g, jj, i, :] = np.where(ok, 0.0, -30000.0)
    k = np.arange(64)[:, None]
    q = np.arange(64)[None, :]
    MBb = np.zeros((64, 12, 3, 64), np.float32)
    MBm = np.zeros((64, 12, 3, 64), np.float32)
    for p, dil in enumerate((1, 4, 16)):
        for jj in range(3):
            rel = (jj - 1) * 64 + k - q
            bk = rel_bucket_np(dil * rel)
            ok = np.abs(rel) <= 64
            for hh in range(4):
                MBb[:, p * 4 + hh, jj, :] = np.where(ok, rel_bias[bk, p * 4 + hh], 0.0)
                MBm[:, p * 4 + hh, jj, :] = np.where(ok, 0.0, -30000.0)
    return (MDb.reshape(128, -1), MDm.reshape(128, -1), MBb.reshape(64, -1), MBm.reshape(64, -1))


def fused_in_maps(inp, layers=(0, 1), xcur=None):
    MDb, MDm, MBb, MBm = bias_tables(inp["rel_bias"])
    ws = [layer_weights_P(inp, l) for l in layers]
    nl = len(layers)
    if xcur is None:
        xcur = inp["x"]
    shared = {}
    for k in ("wt", "gattn", "gqa", "gkva", "gqn", "gkn", "wuq", "wukv"):
        shared[k] = np.ascontiguousarray(np.stack([w[k] for w in ws], 0))
    wo = inp["w_out"][list(layers)].reshape(nl, 36, 128, 16, 256).transpose(0, 3, 2, 1, 4).reshape(nl, 16, 128, 36 * 256)
    shared["wo"] = np.ascontiguousarray(wo)
    shared["sinksb"] = np.ascontiguousarray(np.broadcast_to(inp["sinks"][list(layers)][:, None, :], (nl, 128, 8)), dtype=np.float32)
    shared["gfin"] = np.ascontiguousarray(np.broadcast_to(inp["g_final"][None, :], (128, 4096)), dtype=np.float32)
    shared["MDb"], shared["MDm"], shared["MBb"], shared["MBm"] = MDb, MDm, MBb, MBm
    eye = np.eye(128, dtype=np.float32)
    maps = []
    for c in range(8):
        b, j = divmod(c, 4)
        m = dict(shared)
        m["x"] = np.ascontiguousarray(xcur[b, j * S_LOC:(j + 1) * S_LOC, :])
        m["cosA"], m["sinA"], m["cosC"], m["sinC"] = rope_tables(j)
        m["onesL"] = np.full((128, 128), 1.0 if j > 0 else 0.0, np.float32)
        m["onesR"] = np.full((128, 128), 1.0 if j < 3 else 0.0, np.float32)
        sel = np.zeros((128, 8, 128), np.float32)
        for r in range(4):
            if r == j - 1:
                sel[:, r, :] = eye
            if r == j + 1:
                sel[:, 4 + r, :] = eye
        m["selM"] = sel.reshape(128, 8 * 128)
        maps.append(m)
    return maps


_NC = {}


def kernel(x, g_attn, w_in, g_qa, g_kva, w_uq, w_ukv, g_qn, g_kn, sinks, w_out, rel_bias, g_final):
    f = lambda a: np.asarray(a, np.float32)
    inp = dict(x=f(x), g_attn=f(g_attn), w_in=f(w_in), g_qa=f(g_qa), g_kva=f(g_kva), w_uq=f(w_uq), w_ukv=f(w_ukv),
               g_qn=f(g_qn), g_kn=f(g_kn), sinks=f(sinks), w_out=f(w_out), rel_bias=f(rel_bias), g_final=f(g_final))
    if "nc" not in _NC:
        _NC["nc"] = build_fused(2, "PAT", True)
    res = run_bass_kernel_spmd(_NC["nc"], fused_in_maps(inp, (0, 1)), core_ids=list(range(8))).results
    return np.stack([np.concatenate([np.asarray(res[4 * b + j]["xo"], np.float32) for j in range(4)], 0) for b in range(2)], 0)
```
